# Optimizing a Trainium2 kernel written in Bass

```python
import jax, jax.numpy as jnp
from jax import lax
import numpy as np

D_MODEL = 1024
BATCH = 8
SEQ = 2048
DEPTH = 1
DEC_BATCH = 128
DEC_SEQ = 1
PAST_LEN = 16384
PAGE_SIZE = 128

MIX_WIDTH = D_MODEL
POOL_WINDOWS = (2, 4, 8, 16)
POOL_GROUPS = len(POOL_WINDOWS)
POOL_WIDTH = MIX_WIDTH // 2
POOL_GC = POOL_WIDTH // POOL_GROUPS
POOL_HIST = max(POOL_WINDOWS) - 1
CONV_WIDTH = MIX_WIDTH - POOL_WIDTH
CONV_HEADS = 8
CONV_K = 3
IN_COLS = POOL_WIDTH + 3 * CONV_WIDTH
D_FF = 2816
PLE_DIM = 256
EPS = 1e-6

kernel_name = "hybrid_pool_shortconv_convffn_step"


def rmsnorm(x, g):
    x32 = x.astype(jnp.float32)
    y = x32 * lax.rsqrt(jnp.mean(x32 * x32, axis=-1, keepdims=True) + EPS)
    return (y * g.astype(jnp.float32)).astype(x.dtype)


def causal_dwconv(z, prev, w, b):
    T = z.shape[1]
    ext = jnp.concatenate([prev.astype(z.dtype), z], axis=1)
    y = sum(ext[:, k:k + T] * w[k] for k in range(CONV_K)) + b
    return y, ext[:, -(CONV_K - 1):]


def pool_mix(u, prev, pos, w_pool, scale):
    N, T, _ = u.shape
    ext = jnp.concatenate([prev.astype(u.dtype), u], axis=1)
    cs = jnp.cumsum(ext.astype(jnp.float32), axis=1)
    cs = jnp.concatenate([jnp.zeros((N, 1, POOL_WIDTH), jnp.float32), cs], axis=1)
    end = cs[:, POOL_HIST + 1:]
    means = []
    for g, w in enumerate(POOL_WINDOWS):
        sl = slice(g * POOL_GC, (g + 1) * POOL_GC)
        start = cs[:, POOL_HIST + 1 - w:POOL_HIST + 1 - w + T, sl]
        cnt = jnp.minimum(w, pos + 1).astype(jnp.float32)[None, :, None]
        means.append((end[..., sl] - start) / cnt)
    d = jnp.concatenate(means, axis=-1) - u.astype(jnp.float32)
    d = d.astype(u.dtype).reshape(N, T, POOL_GROUPS, POOL_GC)
    y = jnp.einsum('ntgc,gcd->ntgd', d, w_pool).reshape(N, T, POOL_WIDTH) * scale
    return y, ext[:, -POOL_HIST:]


def layer(r, p, pool_prev, conv_prev, ffn_prev, pos,
          g_mix, w_in, w_pool, pool_scale, conv_w, conv_b, w_out,
          g_ffn, w_up, ffn_conv_w, ffn_conv_b, w_down, g_ple, w_ple_gate, w_ple_proj):
    h = rmsnorm(r, g_mix)
    proj = h @ w_in
    u = proj[..., :POOL_WIDTH]
    bg, cg, hv = jnp.split(proj[..., POOL_WIDTH:], 3, axis=-1)
    y_pool, pool_new = pool_mix(u, pool_prev, pos, w_pool, pool_scale)
    conv_y, conv_new = causal_dwconv(cg * hv, conv_prev, conv_w, conv_b)
    y_conv = bg * conv_y
    r = r + jnp.concatenate([y_pool, y_conv], axis=-1) @ w_out
    h2 = rmsnorm(r, g_ffn)
    up, ffn_new = causal_dwconv(h2 @ w_up, ffn_prev, ffn_conv_w, ffn_conv_b)
    a, v = jnp.split(up, 2, axis=-1)
    r = r + (jax.nn.silu(a) * v) @ w_down
    gate = jax.nn.sigmoid(rmsnorm(r, g_ple) @ w_ple_gate)
    r = r + gate * (p @ w_ple_proj)
    return r, pool_new, conv_new, ffn_new


def setup_inputs(seed: int = 0) -> dict:
    key = jax.random.key(seed)
    ks = iter(jax.random.split(key, 32))
    f32 = jnp.float32
    nrm = lambda shape, s=1.0: (jax.random.normal(next(ks), shape, f32) * s)
    gain = lambda shape: 1.0 + 0.05 * jax.random.normal(next(ks), shape, f32)
    return {
        "x_prompt": nrm((BATCH, SEQ, D_MODEL)),
        "x_sample": nrm((DEC_BATCH, DEC_SEQ, D_MODEL)),
        "state_pool": nrm((DEPTH, DEC_BATCH, POOL_HIST, POOL_WIDTH)),
        "state_conv": nrm((DEPTH, DEC_BATCH, CONV_K - 1, CONV_WIDTH)),
        "state_ffn": nrm((DEPTH, DEC_BATCH, CONV_K - 1, 2 * D_FF)),
        "p_prompt": nrm((DEPTH, BATCH, SEQ, PLE_DIM)),
        "p_sample": nrm((DEPTH, DEC_BATCH, DEC_SEQ, PLE_DIM)),
        "g_mix": gain((DEPTH, D_MODEL)),
        "w_in": nrm((DEPTH, D_MODEL, IN_COLS), D_MODEL ** -0.5),
        "w_pool": nrm((DEPTH, POOL_GROUPS, POOL_GC, POOL_GC), POOL_GC ** -0.5),
        "pool_scale": gain((DEPTH, POOL_WIDTH)),
        "conv_w": nrm((DEPTH, CONV_K, CONV_WIDTH), CONV_K ** -0.5),
        "conv_b": nrm((DEPTH, CONV_WIDTH), 0.02),
        "w_out": nrm((DEPTH, MIX_WIDTH, D_MODEL), MIX_WIDTH ** -0.5),
        "g_ffn": gain((DEPTH, D_MODEL)),
        "w_up": nrm((DEPTH, D_MODEL, 2 * D_FF), D_MODEL ** -0.5),
        "ffn_conv_w": nrm((DEPTH, CONV_K, 2 * D_FF), CONV_K ** -0.5),
        "ffn_conv_b": nrm((DEPTH, 2 * D_FF), 0.02),
        "w_down": nrm((DEPTH, D_FF, D_MODEL), D_FF ** -0.5),
        "g_ple": gain((DEPTH, D_MODEL)),
        "w_ple_gate": nrm((DEPTH, D_MODEL, D_MODEL), D_MODEL ** -0.5),
        "w_ple_proj": nrm((DEPTH, PLE_DIM, D_MODEL), PLE_DIM ** -0.5),
        "g_final": gain((D_MODEL,)),
    }


def reference(x_prompt, x_sample, state_pool, state_conv, state_ffn, p_prompt, p_sample,
              g_mix, w_in, w_pool, pool_scale, conv_w, conv_b, w_out,
              g_ffn, w_up, ffn_conv_w, ffn_conv_b, w_down, g_ple, w_ple_gate, w_ple_proj, g_final):
    nb, tp = x_prompt.shape[0], x_prompt.shape[1]
    ts = x_sample.shape[1]
    pos_p = jnp.arange(tp, dtype=jnp.int32)
    pos_s = PAST_LEN + jnp.arange(ts, dtype=jnp.int32)
    dt = x_prompt.dtype
    rp, rs = x_prompt, x_sample
    pp_l, pc_l, pf_l, sp_l, sc_l, sf_l = [], [], [], [], [], []
    for i in range(DEPTH):
        w = (g_mix[i], w_in[i], w_pool[i], pool_scale[i], conv_w[i], conv_b[i], w_out[i],
             g_ffn[i], w_up[i], ffn_conv_w[i], ffn_conv_b[i], w_down[i],
             g_ple[i], w_ple_gate[i], w_ple_proj[i])
        rp, pp, pc, pf = layer(rp, p_prompt[i],
                               jnp.zeros((nb, POOL_HIST, POOL_WIDTH), dt),
                               jnp.zeros((nb, CONV_K - 1, CONV_WIDTH), dt),
                               jnp.zeros((nb, CONV_K - 1, 2 * D_FF), dt), pos_p, *w)
        rs, sp, sc, sf = layer(rs, p_sample[i], state_pool[i], state_conv[i], state_ffn[i], pos_s, *w)
        pp_l.append(pp); pc_l.append(pc); pf_l.append(pf)
        sp_l.append(sp); sc_l.append(sc); sf_l.append(sf)
    y_prompt = rmsnorm(rp, g_final)
    y_sample = rmsnorm(rs, g_final)
    return (y_prompt, y_sample,
            jnp.stack(pp_l), jnp.stack(pc_l), jnp.stack(pf_l),
            jnp.stack(sp_l), jnp.stack(sc_l), jnp.stack(sf_l))
```

```python
import numpy as np
from contextlib import ExitStack
import concourse.bass as bass
import concourse.mybir as mybir
from concourse.bass_utils import run_bass_kernel_spmd

F32 = mybir.dt.float32
F32R = mybir.dt.float32r
AF = mybir.ActivationFunctionType
ALU = mybir.AluOpType

NCORES = 8
NT = 688
NB = 344
HALO = 16
ROW = HALO + NT
NTILES = 3
NPROMPT = 2048
NS = 16
NCOLS = NPROMPT + NS
NSLOT = 3
SLOTW = 4096
NSCR = 10
EPS = 1e-6
FAST_RECIP = False

GM, GF, GP, GFIN, PSC, CW, CB, FW, FB, NV = 0, 8, 16, 24, 32, 36, 48, 52, 184, 228
POOL_W = (2, 4, 8, 16)


PARTS3 = ((0, 8), (8, 14), (14, 22))


def ffn_sequence():
    seq = []
    for pi, (j0, j1) in enumerate(PARTS3):
        ups = list(range(j0, j1, 2))
        for jj in ups[(1 if pi > 0 else 0):]:
            seq.append(("up", pi, jj, False))
        if pi + 1 < len(PARTS3):
            seq.append(("up", pi + 1, PARTS3[pi + 1][0], True))
        seq.append(("down", pi))
    return seq


class Prog:
    ENGS = ("pe", "act", "dve", "pool", "sp")
    STRICT = ("pool", "dve", "act")

    def __init__(self):
        self.streams = {e: [] for e in self.ENGS}
        self.count = {}
        self.waited = {e: {} for e in self.ENGS}
        self.last_write = {}
        self.readers = {}

    def _deps(self, eng, r, w):
        deps = []
        for k in r:
            t = self.last_write.get(k)
            if t is not None:
                deps.append(t)
        strict = eng in self.STRICT
        for k in w:
            t = self.last_write.get(k)
            if t is not None and (strict or t[2] != eng):
                deps.append(t)
            for t2 in self.readers.get(k, ()):
                if strict or t2[2] != eng:
                    deps.append(t2)
        best = {}
        for (s, v, e) in deps:
            if eng == "pe" and e == "pe":
                continue
            if v > best.get(s, 0):
                best[s] = v
        waits = []
        for s, v in best.items():
            if self.waited[eng].get(s, 0) < v:
                self.waited[eng][s] = v
                waits.append((s, v))
        return waits

    def op(self, eng, fn, r=(), w=(), sem=None, inc=1, tok_eng=None):
        waits = self._deps(eng, r, w)
        semname = sem if sem is not None else "s_" + eng
        v = self.count.get(semname, 0) + inc
        self.count[semname] = v
        tok = (semname, v, tok_eng if tok_eng is not None else eng)
        self.streams[eng].append((waits, fn, semname, inc))
        for k in r:
            self.readers.setdefault(k, []).append(tok)
        for k in w:
            self.last_write[k] = tok
            self.readers[k] = []
        return tok

    def dma(self, queue, sem, fn, r=(), w=()):
        return self.op(queue, fn, r, w, sem=sem, inc=16, tok_eng="dma:" + sem)


def build_program():
    nc = bass.Bass("TRN2", target_bir_lowering=False)
    P = Prog()

    def din(name, shape):
        return nc.dram_tensor(name, shape, F32, kind="ExternalInput").ap()

    def dout(name, shape):
        return nc.dram_tensor(name, shape, F32, kind="ExternalOutput").ap()

    xT = din("xT", [1024, NCOLS])
    pT = din("pT", [256, NCOLS]).bitcast(F32R)
    vec_d = din("vec", [128, NV])
    sp_d = din("sp_in", [128, 15, 4, NS])
    sc_d = din("sc_in", [128, 2, 4, NS])
    sf_d = din("sf_in", [128, 2, 44, NS])
    w_inu_d = din("w_in_u", [1024, 512]).bitcast(F32R)
    w_inc_d = din("w_in_c", [4, 1024, 384]).bitcast(F32R)
    w_pool_d = din("w_pool", [4, 128, 128]).bitcast(F32R)
    w_out_d = din("w_out", [1024, 1024]).bitcast(F32R)
    w_upu_d = din("w_up_u", [11, 1024, 512]).bitcast(F32R)
    w_down_d = din("w_down", [2816, 1024]).bitcast(F32R)
    w_gate_d = din("w_gate", [1024, 1024]).bitcast(F32R)
    w_proj_d = din("w_proj", [256, 1024]).bitcast(F32R)

    yT = dout("yT", [1024, NCOLS])
    npp_d = dout("npp", [128, 4, 15])
    ncp_d = dout("ncp", [128, 4, 2])
    nfp_d = dout("nfp", [128, 44, 2])
    nps_d = dout("nps", [128, 15, 4, NS])
    ncs_d = dout("ncs", [128, 2, 4, NS])
    nfs_d = dout("nfs", [128, 2, 44, NS])

    es = ExitStack()
    with es:
        def sb(name, shape, dt=F32):
            return es.enter_context(nc.sbuf_tensor(name, shape, dt))

        RB = [sb("R0", [128, 8, NT]), sb("R1", [128, 8, NT])]
        H = sb("H", [128, 8, NT], F32R)
        GB = sb("GB", [128, 8, NT], F32R)
        PT = sb("PT", [128, 2, NT], F32R)
        SCR = sb("SCR", [128, NSCR, ROW])
        DROW = sb("DROW", [128, 2, NT], F32R)
        WS = sb("WS", [128, NSLOT, SLOTW], F32R)
        WP = sb("WP", [128, 4, 128], F32R)
        VEC = sb("VEC", [128, NV])
        ONESF = sb("ONESF", [128, 128])
        ONES = sb("ONES", [128, 128], F32R)
        EPSV = sb("EPSV", [128, 1])
        DUMMY = sb("DUMMY", [128, 2])
        SPs = sb("SPs", [128, 15, 4, NS])
        SCs = sb("SCs", [128, 2, 4, NS])
        SFs = sb("SFs", [128, 2, 44, NS])
        UPL = sb("UPL", [128, 44, 2])
        CHL = sb("CHL", [128, 4, 2])
        CHS = sb("CHS", [128, 4, NS])
        UL = sb("UL", [128, 4, 15])
        US = sb("US", [128, 4, NS])
        BASEC = sb("BASEC", [128, 4, NS])
        POOLS = sb("POOLS", [128, 4, NS])
        PACC = sb("PACC", [128, 4, NS])
        INVC = sb("INVC", [128, 4, 16])
        PS = es.enter_context(nc.psum_tensor("PS", [128, 8, 512], F32))
        BASES = SFs[:, 0, :, :]
        UPS = SFs[:, 1, :, :]

        sem_names = ["s_pe", "s_act", "s_dve", "s_pool", "semR0", "semR1", "semPT", "semC", "semWP",
                     "semY", "semO"] + ["semX%d" % k for k in range(8)] + \
                    ["semW%d_%d" % (i, q) for i in range(NSLOT) for q in range(3)]
        sems = {n: es.enter_context(nc.semaphore(n)) for n in sem_names}

        NROT = NSCR - 1
        RSTD1 = NSCR - 1

        def vcol(c):
            return VEC[:, c:c + 1]

        def blk2(ap):
            return ap.rearrange("p (b n) -> p b n", b=2)

        def psv(s):
            return PS[:, 2 * s:2 * s + 2, 0:NB]

        def scr(i, a=HALO, b=ROW):
            return SCR[:, i, a:b]

        rr = {"ps": 0, "scr": 0}
        reserved = set()

        def next_ps():
            while True:
                s = rr["ps"]
                rr["ps"] = (s + 1) % 4
                if s not in reserved:
                    return s

        def next_scr():
            s = rr["scr"]
            rr["scr"] = (s + 1) % NROT
            return s

        def kscr(i):
            return [("scr", i), ("scrh", i)]

        def src_cols(wd, a, b):
            return wd[:, a:b].rearrange("(k p) o -> p k o", p=128)

        units = []

        def add_unit(nk, ntot, parts):
            units.append((nk, ntot, parts))

        for t in range(NTILES):
            add_unit(8, 512, [(0, 512, src_cols(w_inu_d, 0, 512))])
            for j in range(4):
                add_unit(8, 384, [(0, 384, w_inc_d[j].rearrange("(k p) o -> p k o", p=128))])
            for ou in range(2):
                add_unit(8, 512, [(0, 512, src_cols(w_out_d, 512 * ou, 512 * ou + 512))])
            for item in ffn_sequence():
                if item[0] == "up":
                    jj = item[2]
                    add_unit(8, 512, [(0, 512, w_upu_d[jj // 2].rearrange("(k p) o -> p k o", p=128))])
                else:
                    j0, j1 = PARTS3[item[1]]
                    nk = j1 - j0
                    for ou in range(2):
                        add_unit(nk, 512, [(0, 512, w_down_d[128 * j0:128 * j1, 512 * ou:512 * ou + 512]
                                            .rearrange("(k p) o -> p k o", p=128))])
            add_unit(8, 512, [(0, 512, src_cols(w_gate_d, 0, 512))])
            add_unit(2, 1024, [(0, 1024, src_cols(w_proj_d, 0, 1024))])
            add_unit(8, 512, [(0, 512, src_cols(w_gate_d, 512, 1024))])

        ustate = {"issued": 0, "next": 0, "done": set()}

        def uview(i):
            nk, ntot, _ = units[i]
            return WS[:, i % NSLOT, 0:nk * ntot].rearrange("p (k o) -> p k o", k=nk)

        def ukeys(i):
            return [("w", i % NSLOT, q) for q in range(3)]

        def issue_units(limit=None):
            while ustate["issued"] < len(units):
                if limit is not None and ustate["issued"] >= limit:
                    break
                i = ustate["issued"]
                if i >= NSLOT and (i - NSLOT) not in ustate["done"]:
                    break
                nk, ntot, parts = units[i]
                v = uview(i)
                slot = i % NSLOT
                for q, (off, ncol, src) in enumerate(parts):
                    if len(parts) == 1:
                        keys = ukeys(i)
                    elif q == len(parts) - 1:
                        keys = [("w", slot, qq) for qq in range(q, 3)]
                    else:
                        keys = [("w", slot, q)]
                    dst = v[:, :, off:off + ncol]
                    P.dma("pool", "semW%d_%d" % (slot, q),
                          (lambda e, dst=dst, src=src: e.dma_start(out=dst, in_=src)),
                          r=(), w=keys)
                ustate["issued"] += 1

        def take_unit():
            i = ustate["next"]
            ustate["next"] += 1
            assert i < ustate["issued"], "unit not issued (prefetch logic)"
            return i

        def unit_done(i):
            ustate["done"].add(i)
            issue_units()

        def mm_group(s, lhs_fn, rhs_fn, nk, rkeys):
            items = []
            for b in range(2):
                for k in range(nk):
                    items.append((PS[:, 2 * s + b, 0:NB], lhs_fn(k), rhs_fn(k, b), k == 0, k == nk - 1))

            def fn(e):
                ins = None
                for (o, l, r_, st, sp) in items:
                    ins = e.matmul(o, l, r_, start=st, stop=sp)
                return ins
            P.op("pe", fn, r=rkeys, w=[("ps", s)])

        def mm_kouter(slots, lhs_fns, rhs_fn, nk, rkeys_k, rkeys_common):
            for k in range(nk):
                items = []
                for ci, s in enumerate(slots):
                    for b in range(2):
                        items.append((PS[:, 2 * s + b, 0:NB], lhs_fns[ci](k), rhs_fn(k, b), k == 0, k == nk - 1))

                def fn(e, items=items):
                    ins = None
                    for (o, l, r_, st, sp) in items:
                        ins = e.matmul(o, l, r_, start=st, stop=sp)
                    return ins
                P.op("pe", fn, r=[rkeys_k(k)] + rkeys_common, w=[("ps", s) for s in slots])

        def act(out, in_, func, r, w, **kw):
            P.op("act", (lambda e: e.activation(out=out, in_=in_, func=func, **kw)), r=r, w=w)

        def tt(out, in0, in1, op, r, w):
            P.op("dve", (lambda e: e.tensor_tensor(out=out, in0=in0, in1=in1, op=op)), r=r, w=w)

        def stt(out, in0, scalar, in1, op0, op1, r, w):
            P.op("dve", (lambda e: e.scalar_tensor_tensor(out=out, in0=in0, scalar=scalar, in1=in1,
                                                          op0=op0, op1=op1)), r=r, w=w)

        def pcopy(out, in_, r, w):
            P.op("pool", (lambda e: e.tensor_copy(out=out, in_=in_)), r=r, w=w)

        P.dma("sp", "semC", lambda e: e.dma_start(out=VEC[:, :], in_=vec_d), w=["VEC"])
        for k in range(8):
            P.dma("sp", "semX%d" % k, (lambda e, k=k: e.dma_start(
                out=RB[0][:, k, :], in_=xT[128 * k:128 * k + 128, 0:NT])), w=[("R", 0, k)])
        P.op("pool", lambda e: e.memset(ONESF[:, :], 1.0), w=["ONESF"])
        P.op("pool", lambda e: e.memset(EPSV[:, :], EPS), w=["EPSV"])
        P.op("act", lambda e: e.activation(out=ONES[:, :], in_=ONESF[:, :], func=AF.Copy),
             r=["ONESF"], w=["ONES"])
        issue_units(limit=1)
        P.dma("pool", "semWP", lambda e: e.dma_start(out=WP[:, :, :],
                                                      in_=w_pool_d.rearrange("g c d -> c g d")), w=["WP"])
        P.op("pool", lambda e: e.memset(UPL[:, :, :], 0.0), w=[("UPL", c) for c in range(44)])
        P.op("pool", lambda e: e.memset(CHL[:, :, :], 0.0), w=[("CHL", c) for c in range(4)])
        P.op("pool", lambda e: e.memset(UL[:, :, :], 0.0), w=[("UL", c) for c in range(4)])
        for g, wdw in enumerate(POOL_W):
            for tcol in range(16):
                val = 1.0 / min(wdw, tcol + 1)
                P.op("pool", (lambda e, g=g, tcol=tcol, val=val: e.memset(INVC[:, g, tcol:tcol + 1], val)),
                     w=[("INVC", g)])
        issue_units()
        P.dma("pool", "semPT", lambda e: e.dma_start(
            out=PT[:, :, :], in_=pT[:, 0:NT].rearrange("(k p) n -> p k n", p=128)),
            w=[("PT", 0), ("PT", 1)])
        P.dma("sp", "semC", lambda e: e.dma_start(out=SPs[:, :, :, :], in_=sp_d), w=["SPs"])
        P.dma("sp", "semC", lambda e: e.dma_start(out=SCs[:, :, :, :], in_=sc_d), w=["SCs"])
        P.dma("sp", "semC", lambda e: e.dma_start(out=SFs[:, :, :, :], in_=sf_d), w=["SFs"])
        for kk in ("VEC", "SPs", "SCs", "SFs"):
            P.last_write[kk] = ("semC", P.count["semC"], "dma:semC")
        P.dma("sp", "semO", lambda e: e.dma_start(out=nfs_d[:, 0, :, :], in_=sf_d[:, 1, :, :]))

        def build_precompute_pieces():
            pieces = []

            def p_pools():
                act(PACC[:, :, :], SPs[:, 14, :, :], AF.Copy, r=["SPs"], w=["PACC"])
                act(POOLS[:, 0, :], PACC[:, 0, :], AF.Copy, r=["PACC"], w=[("POOLS", 0)])
            pieces.append(p_pools)
            row = 13
            for g in range(1, 4):
                lo = 15 - (POOL_W[g] - 1)
                rows = list(range(row, lo - 1, -1))
                row = lo - 1

                def p_g(g=g, rows=rows):
                    for rw in rows:
                        tt(PACC[:, :, :], PACC[:, :, :], SPs[:, rw, :, :], ALU.add, r=["PACC", "SPs"], w=["PACC"])
                    act(POOLS[:, g, :], PACC[:, g, :], AF.Copy, r=["PACC"], w=[("POOLS", g)])
                pieces.append(p_g)

            def p_c(j):
                act(BASEC[:, j, :], SCs[:, 0, j, :], AF.Identity, r=["SCs", "VEC"], w=[("CHS_base", j)],
                    scale=vcol(CW + j))
                stt(BASEC[:, j, :], SCs[:, 1, j, :], vcol(CW + 4 + j), BASEC[:, j, :], ALU.mult, ALU.add,
                    r=["SCs", "VEC", ("CHS_base", j)], w=[("CHS_base", j)])

            def p_s(c):
                act(BASES[:, c, :], SFs[:, 0, c, :], AF.Identity, r=["SFs", "VEC"], w=[("UPS_base", c)],
                    scale=vcol(FW + c))
                stt(BASES[:, c, :], SFs[:, 1, c, :], vcol(FW + 44 + c), BASES[:, c, :], ALU.mult, ALU.add,
                    r=["SFs", "VEC", ("UPS_base", c), ("UPS", c)], w=[("UPS_base", c)])
            for j in range(4):
                pieces.append(lambda j=j: p_c(j))
            for c in range(44):
                pieces.append(lambda c=c: p_s(c))
            return pieces

        pre_pieces = build_precompute_pieces()

        class NormAcc:
            def __init__(self, rb, sq_fn, maxpend=8):
                self.rb = rb
                self.maxpend = maxpend
                self.sq_fn = sq_fn
                self.s = next_ps()
                reserved.add(self.s)
                self.cnt = 0
                self.pending = []

            def square(self, k):
                while len(self.pending) >= self.maxpend:
                    self.mm()
                ap, key = self.sq_fn(k)
                act(ap, RB[self.rb][:, k, :], AF.Square, r=[("R", self.rb, k)], w=[key])
                self.pending.append(k)

            def mm(self):
                if not self.pending:
                    return
                k = self.pending.pop(0)
                ap, key = self.sq_fn(k)
                first, last = self.cnt == 0, self.cnt == 7
                self.cnt += 1
                s = self.s

                def fn(e):
                    ins = None
                    for b in range(2):
                        ins = e.matmul(PS[:, 2 * s + b, 0:NB], ONES[:, :], ap[:, b * NB:(b + 1) * NB],
                                       start=first, stop=last)
                    return ins
                P.op("pe", fn, r=[key, "ONES"], w=[("ps", s)])

            def finish(self, rs):
                while self.pending:
                    self.mm()
                assert self.cnt == 8
                sd = next_scr()
                act(blk2(scr(sd)), psv(self.s), AF.Ln, r=[("ps", self.s), "EPSV"], w=[("scr", sd)],
                    bias=EPSV[:, 0:1], scale=1.0 / 1024.0)
                reserved.discard(self.s)
                act(scr(rs), scr(sd), AF.Exp, r=[("scr", sd)], w=[("scr", rs)], scale=-0.5)
                return rs

        def preload_ln_table():
            act(DUMMY[:, 0:1], EPSV[:, 0:1], AF.Ln, r=["EPSV"], w=["DUMMY"])

        def norm_apply(rb, gcol, rs, to_y=False):
            for k in range(8):
                out = RB[rb][:, k, :] if to_y else H[:, k, :]
                stt(out, RB[rb][:, k, :], vcol(gcol + k), scr(rs), ALU.mult, ALU.mult,
                    r=[("R", rb, k), ("scr", rs), "VEC"], w=[("R", rb, k) if to_y else ("H", k)])

        def sq_H(k):
            return H[:, k, :], ("H", k)

        def sq_GB(k):
            return GB[:, k, :], ("GB", k)

        def sq_D(k):
            return DROW[:, k % 2, :], ("DROW", k % 2)

        def conv_from_row(src, c, wcol, bcol, nch, base_t, key_l, L, key_s, S_, npr, has_s):
            pcopy(SCR[:, src, HALO - 2:HALO], L[:, c, :], r=[(key_l, c)], w=[("scrh", src)])
            pcopy(L[:, c, :], SCR[:, src, HALO + npr - 2:HALO + npr], r=[("scr", src)], w=[(key_l, c)])
            if has_s:
                pcopy(S_[:, c, :], SCR[:, src, HALO + npr:ROW], r=[("scr", src)], w=[(key_s, c)])
            acc = next_scr()
            act(scr(acc), scr(src), AF.Identity, r=[("scr", src), "VEC"], w=[("scr", acc)],
                scale=vcol(wcol + 2 * nch + c), bias=vcol(bcol + c))
            stt(SCR[:, acc, HALO:HALO + npr], SCR[:, src, HALO - 1:HALO + npr - 1], vcol(wcol + nch + c),
                SCR[:, acc, HALO:HALO + npr], ALU.mult, ALU.add,
                r=kscr(src) + [("scr", acc), "VEC"], w=[("scr", acc)])
            stt(SCR[:, acc, HALO:HALO + npr], SCR[:, src, HALO - 2:HALO + npr - 2], vcol(wcol + c),
                SCR[:, acc, HALO:HALO + npr], ALU.mult, ALU.add,
                r=kscr(src) + [("scr", acc), "VEC"], w=[("scr", acc)])
            if has_s:
                tt(SCR[:, acc, HALO + npr:ROW], SCR[:, acc, HALO + npr:ROW], base_t[:, c, :], ALU.add,
                   r=[("scr", acc), (key_s + "_base", c)], w=[("scr", acc)])
            return acc

        state = {"rs1": None, "n4": None, "ytodo": []}

        def proj_residual(rb, nk, src_key, src_fn, norm_make, hg_col=None):
            R = RB[rb]
            u0 = take_unit()
            W0 = uview(u0)
            u1 = None
            W1 = None
            skeys = [(src_key, k) for k in range(nk)]
            KO = 3
            sl = [next_ps() for _ in range(KO)]
            mm_kouter(sl, [(lambda k, q=q: W0[:, k, q * 128:(q + 1) * 128]) for q in range(KO)],
                      src_fn, nk, rkeys_k=lambda k: (src_key, k), rkeys_common=ukeys(u0))
            na = None
            for oc in range(8):
                q = oc % 4
                if oc == 4:
                    unit_done(u0)
                    u1 = take_unit()
                    W1 = uview(u1)
                if oc < KO:
                    s = sl[oc]
                else:
                    Wc_, uc_ = (W0, u0) if oc < 4 else (W1, u1)
                    s = next_ps()
                    mm_group(s, lambda k, q=q, Wc_=Wc_: Wc_[:, k, q * 128:(q + 1) * 128], src_fn, nk,
                             rkeys=skeys + ukeys(uc_))
                if na is not None and oc >= KO:
                    na.mm()
                    if len(na.pending) > 2:
                        na.mm()
                tt(blk2(R[:, oc, :]), psv(s), blk2(R[:, oc, :]), ALU.add,
                   r=[("ps", s), ("R", rb, oc)], w=[("R", rb, oc)])
                if na is None and norm_make is not None and oc >= 1:
                    na = norm_make()
                    for o2 in range(oc):
                        na.square(o2)
                if na is not None:
                    na.square(oc)
                if hg_col is not None:
                    act(H[:, oc, :], R[:, oc, :], AF.Identity, r=[("R", rb, oc), "VEC"], w=[("H", oc)],
                        scale=vcol(hg_col + oc))
                if pre_pieces and oc < 7:
                    pre_pieces.pop(0)()
            unit_done(u1)
            return na

        def emit_state_outputs():
            def out_dma(dst, src, r):
                P.dma("sp", "semO", (lambda e: e.dma_start(out=dst, in_=src)), r=r)

            out_dma(npp_d, UL[:, :, :], [("UL", g) for g in range(4)])
            out_dma(ncp_d, CHL[:, :, :], [("CHL", g) for g in range(4)])
            out_dma(nfp_d, UPL[:, :, :], [("UPL", c) for c in range(44)])
            out_dma(nps_d[:, 0:14, :, :], SPs[:, 1:15, :, :], ["SPs"])
            out_dma(nps_d[:, 14, :, :], US[:, :, :], [("US", g) for g in range(4)])
            out_dma(ncs_d[:, 0, :, :], SCs[:, 1, :, :], ["SCs"])
            out_dma(ncs_d[:, 1, :, :], CHS[:, :, :], [("CHS", g) for g in range(4)])
            out_dma(nfs_d[:, 1, :, :], UPS, [("UPS", c) for c in range(44)])

        def tile_head(t):
            rb = t % 2
            if t == 0:
                na = NormAcc(0, sq_H)
                for k in range(8):
                    na.square(k)
                    na.mm()
                na.finish(RSTD1)
            norm_apply(rb, GM, RSTD1)

        def tile_tail(t):
            rb = t % 2
            c0 = NT * t
            rs = next_scr()
            state["n4"].finish(rs)
            for k in range(8):
                stt(RB[rb][:, k, :], RB[rb][:, k, :], vcol(GFIN + k), scr(rs), ALU.mult, ALU.mult,
                    r=[("R", rb, k), ("scr", rs), "VEC"], w=[("R", rb, k)])
                P.dma("sp", "semX%d" % k, (lambda e, k=k: e.dma_start(
                    out=yT[128 * k:128 * k + 128, c0:c0 + NT], in_=RB[rb][:, k, :])),
                    r=[("R", rb, k)])

        def tail_start(t):
            state["n4"].finish(RSTD1)
            state["ytodo"] = [(t, k) for k in range(8)]

        def tail_piece(n):
            for _ in range(n):
                if not state["ytodo"]:
                    return
                t, k = state["ytodo"].pop(0)
                rb = t % 2
                stt(RB[rb][:, k, :], RB[rb][:, k, :], vcol(GFIN + k), scr(RSTD1), ALU.mult, ALU.mult,
                    r=[("R", rb, k), ("scr", RSTD1), "VEC"], w=[("R", rb, k)])
                if not state["ytodo"]:
                    c0 = NT * t
                    P.dma("sp", "semY", lambda e: e.dma_start(
                        out=yT[:, c0:c0 + NT].rearrange("(k p) n -> p k n", p=128), in_=RB[rb][:, :, :]),
                        r=[("R", rb, k) for k in range(8)])

        def tile_body(t):
            rb = t % 2
            R = RB[rb]
            has_s = (t == NTILES - 1)
            npr = NT - NS if has_s else NT
            hk = [("H", k) for k in range(8)]

            def rk(k):
                return ("R", rb, k)

            def hrhs(k, b):
                return H[:, k, b * NB:(b + 1) * NB]

            ui = take_unit()
            Wu = uview(ui)
            uslots = [next_ps() for _ in range(4)]
            mm_kouter(uslots, [(lambda k, g=g: Wu[:, k, g * 128:(g + 1) * 128]) for g in range(4)],
                      hrhs, 8, rkeys_k=lambda k: ("H", k), rkeys_common=ukeys(ui))
            unit_done(ui)
            for s_ in uslots:
                reserved.add(s_)

            def pool_chain(g):
                wdw = POOL_W[g]
                s = uslots[g]
                U = next_scr()
                act(blk2(scr(U)), psv(s), AF.Copy, r=[("ps", s)], w=[("scr", U)])
                reserved.discard(s)
                pcopy(SCR[:, U, 1:HALO], UL[:, g, :], r=[("UL", g)], w=[("scrh", U)])
                pcopy(UL[:, g, :], SCR[:, U, HALO + npr - 15:HALO + npr], r=[("scr", U)], w=[("UL", g)])
                if has_s:
                    pcopy(US[:, g, :], SCR[:, U, HALO + npr:ROW], r=[("scr", U)], w=[("US", g)])
                A = next_scr()
                B = next_scr()
                cur, sh, lo = U, 1, 2
                dst = A
                for lvl in range(g + 1):
                    tt(SCR[:, dst, lo:ROW], SCR[:, cur, lo:ROW], SCR[:, cur, lo - sh:ROW - sh], ALU.add,
                       r=kscr(cur), w=kscr(dst))
                    cur = dst
                    dst = B if cur == A else A
                    sh *= 2
                    lo *= 2
                Srow = cur
                di = g
                stt(GB[:, g, :], scr(Srow), 1.0 / wdw, scr(U), ALU.mult, ALU.subtract,
                    r=kscr(Srow) + kscr(U), w=[("GB", g)])
                tmp = dst
                if t == 0:
                    tt(SCR[:, tmp, 0:15], SCR[:, Srow, HALO:HALO + 15], INVC[:, g, 0:15], ALU.mult,
                       r=kscr(Srow) + [("INVC", g)], w=kscr(tmp))
                    tt(GB[:, g, 0:15], SCR[:, tmp, 0:15], SCR[:, U, HALO:HALO + 15], ALU.subtract,
                       r=kscr(tmp) + kscr(U) + [("GB", g)], w=[("GB", g)])
                if has_s:
                    tt(SCR[:, tmp, 0:NS], SCR[:, U, HALO + npr:ROW], POOLS[:, g, :], ALU.add,
                       r=kscr(U) + [("POOLS", g)], w=kscr(tmp))
                    stt(GB[:, g, npr:NT], SCR[:, tmp, 0:NS], 1.0 / wdw, SCR[:, U, HALO + npr:ROW],
                        ALU.mult, ALU.subtract, r=kscr(tmp) + kscr(U) + [("GB", g)], w=[("GB", g)])
                return di

            def pool_mm(g, di):
                s2 = next_ps()
                mm_group(s2, lambda k: WP[:, g, :], lambda k, b: GB[:, g, b * NB:(b + 1) * NB], 1,
                         rkeys=[("GB", g), "WP"])
                act(blk2(GB[:, g, :]), psv(s2), AF.Identity, r=[("ps", s2), "VEC"], w=[("GB", g)],
                    scale=vcol(PSC + g))

            def conv_unit(j):
                ci = take_unit()
                Wc = uview(ci)
                s_cg = next_ps()
                mm_group(s_cg, lambda k: Wc[:, k, 128:256], hrhs, 8, rkeys=hk + ukeys(ci))
                s_hv = next_ps()
                mm_group(s_hv, lambda k: Wc[:, k, 256:384], hrhs, 8, rkeys=hk + ukeys(ci))
                CG = next_scr()
                act(blk2(scr(CG)), psv(s_cg), AF.Copy, r=[("ps", s_cg)], w=[("scr", CG)])
                s_bg = next_ps()
                mm_group(s_bg, lambda k: Wc[:, k, 0:128], hrhs, 8, rkeys=hk + ukeys(ci))
                CH = next_scr()
                tt(blk2(scr(CH)), psv(s_hv), blk2(scr(CG)), ALU.mult, r=[("ps", s_hv), ("scr", CG)],
                   w=[("scr", CH)])
                acc = conv_from_row(CH, j, CW, CB, 4, BASEC, "CHL", CHL, "CHS", CHS, npr, has_s)
                tt(blk2(GB[:, 4 + j, :]), psv(s_bg), blk2(scr(acc)), ALU.mult,
                   r=[("ps", s_bg), ("scr", acc)], w=[("GB", 4 + j)])
                unit_done(ci)

            d0 = pool_chain(0)
            d1 = pool_chain(1)
            conv_unit(0)
            pool_mm(0, d0)
            pool_mm(1, d1)
            tail_piece(2)
            d2 = pool_chain(2)
            d3 = pool_chain(3)
            conv_unit(1)
            tail_piece(2)
            conv_unit(2)
            pool_mm(2, d2)
            pool_mm(3, d3)
            tail_piece(2)
            conv_unit(3)
            tail_piece(2)
            n2 = proj_residual(rb, 8, "GB", lambda k, b: GB[:, k, b * NB:(b + 1) * NB],
                               lambda: NormAcc(rb, sq_H))
            rs2 = next_scr()
            n2.finish(rs2)
            norm_apply(rb, GF, rs2)

            if t + 1 < NTILES:
                nb_ = (t + 1) % 2
                c1 = NT * (t + 1)
                P.dma("sp", "semR%d" % nb_, lambda e: e.dma_start(
                    out=RB[nb_][:, :, :], in_=xT[:, c1:c1 + NT].rearrange("(k p) n -> p k n", p=128)),
                    w=[("R", nb_, k) for k in range(8)])

            parts3 = PARTS3
            n1 = None
            n3 = None
            deferred = []

            def flush_deferred():
                while deferred:
                    deferred.pop(0)()

            def up_unit(pi, jj, pulled):
                j0 = parts3[pi][0]
                fi = take_unit()
                Wf = uview(fi)
                lhs4 = [(lambda k, q=q: Wf[:, k, q * 128:(q + 1) * 128]) for q in range(2)] + \
                       [(lambda k, q=q: Wf[:, k, 256 + q * 128:256 + (q + 1) * 128]) for q in range(2)]
                if jj == 0:
                    sl2 = [next_ps() for _ in range(2)]
                    mm_kouter(sl2, [lhs4[0], lhs4[2]], hrhs, 8,
                              rkeys_k=lambda k: ("H", k), rkeys_common=ukeys(fi))
                    pre = [(sl2[0], sl2[1])]
                else:
                    pre = None
                for q in range(2):
                    if pre is not None and q < len(pre):
                        s_a, s_v = pre[q]
                    else:
                        s_a = next_ps()
                        mm_group(s_a, lhs4[q], hrhs, 8, rkeys=hk + ukeys(fi))
                        s_v = next_ps()
                        mm_group(s_v, lhs4[2 + q], hrhs, 8, rkeys=hk + ukeys(fi))
                    j = jj + q
                    accs = []
                    for (s_x, c) in ((s_a, j), (s_v, 22 + j)):
                        UP = next_scr()
                        act(blk2(scr(UP)), psv(s_x), AF.Copy, r=[("ps", s_x)], w=[("scr", UP)])
                        accs.append(conv_from_row(UP, c, FW, FB, 44, BASES, "UPL", UPL, "UPS", UPS,
                                                  npr, has_s))
                    a_, v_ = accs
                    act(scr(a_), scr(a_), AF.Silu, r=[("scr", a_)], w=[("scr", a_)])
                    if j == 21 or (j == 15 and t + 1 < NTILES):
                        preload_ln_table()

                    def gate_mul(a_=a_, v_=v_, jo=j - j0):
                        tt(GB[:, jo, :], scr(a_), scr(v_), ALU.mult,
                           r=[("scr", a_), ("scr", v_)], w=[("GB", jo)])
                    if pulled:
                        deferred.append(gate_mul)
                    else:
                        gate_mul()
                unit_done(fi)

            for item in ffn_sequence():
                if item[0] == "up":
                    up_unit(item[1], item[2], item[3])
                    continue
                pi = item[1]
                j0, j1 = parts3[pi]
                nk = j1 - j0
                mk = None
                if pi == 1 and t + 1 < NTILES:
                    mk = lambda: NormAcc((t + 1) % 2, sq_D, maxpend=2)
                if pi == 2:
                    mk = lambda: NormAcc(rb, sq_D, maxpend=2)
                na_ = proj_residual(rb, nk, "GB", lambda k, b: GB[:, k, b * NB:(b + 1) * NB], mk,
                                    hg_col=(GP if pi == 2 else None))
                if pi == 1:
                    n1 = na_
                if pi == 2:
                    n3 = na_
                if pi == 1 and n1 is not None:
                    n1.finish(RSTD1)
                flush_deferred()
            rs3 = next_scr()
            n3.finish(rs3)
            if t == NTILES - 1:
                emit_state_outputs()
            assert not (t == 1 and pre_pieces), "precompute pieces left over"

            g0 = take_unit()
            pj = take_unit()
            g1 = take_unit()
            Wpj = uview(pj)
            Wg0 = uview(g0)
            Wg1 = uview(g1)
            ppk = [("PT", 0), ("PT", 1)] + ukeys(pj)
            KG = 3
            gs4 = [next_ps() for _ in range(KG)]
            mm_kouter(gs4, [(lambda k, q=q: Wg0[:, k, q * 128:(q + 1) * 128]) for q in range(KG)], hrhs, 8,
                      rkeys_k=lambda k: ("H", k), rkeys_common=ukeys(g0))
            gts = [next_scr() for _ in range(4)]
            for q in range(KG):
                GT = gts[q]
                tt(blk2(scr(GT)), psv(gs4[q]), blk2(scr(rs3)), ALU.mult, r=[("ps", gs4[q]), ("scr", rs3)],
                   w=[("scr", GT)])
                act(scr(GT), scr(GT), AF.Sigmoid, r=[("scr", GT)], w=[("scr", GT)])
            n4 = None
            for oc in range(8):
                if oc == 4:
                    unit_done(g0)
                if oc >= KG:
                    q = oc % 4
                    Wg_, ug_ = (Wg0, g0) if oc < 4 else (Wg1, g1)
                    sg = next_ps()
                    mm_group(sg, lambda k, q=q, Wg_=Wg_: Wg_[:, k, q * 128:(q + 1) * 128], hrhs, 8,
                             rkeys=hk + ukeys(ug_))
                sp_ = next_ps()
                mm_group(sp_, lambda k, oc=oc: Wpj[:, k, oc * 128:(oc + 1) * 128],
                         lambda k, b: PT[:, k, b * NB:(b + 1) * NB], 2, rkeys=ppk)
                if n4 is not None and oc >= 3:
                    n4.mm()
                GT = gts[oc % 4]
                if oc >= KG:
                    tt(blk2(scr(GT)), psv(sg), blk2(scr(rs3)), ALU.mult, r=[("ps", sg), ("scr", rs3)],
                       w=[("scr", GT)])
                    act(scr(GT), scr(GT), AF.Sigmoid, r=[("scr", GT)], w=[("scr", GT)])
                    if oc == 7:
                        preload_ln_table()
                tt(blk2(scr(GT)), psv(sp_), blk2(scr(GT)), ALU.mult, r=[("ps", sp_), ("scr", GT)],
                   w=[("scr", GT)])
                tt(R[:, oc, :], R[:, oc, :], scr(GT), ALU.add, r=[rk(oc), ("scr", GT)], w=[rk(oc)])
                if n4 is None and oc >= 1:
                    n4 = NormAcc(rb, sq_GB)
                    state["n4"] = n4
                    for o2 in range(oc):
                        n4.square(o2)
                if n4 is not None:
                    n4.square(oc)
            unit_done(pj)
            unit_done(g1)
            if t + 1 < NTILES:
                c1 = NT * (t + 1)
                P.dma("pool", "semPT", lambda e: e.dma_start(
                    out=PT[:, :, :], in_=pT[:, c1:c1 + NT].rearrange("(k p) n -> p k n", p=128)),
                    w=[("PT", 0), ("PT", 1)])

        for t in range(NTILES):
            tile_head(t)
            if t > 0:
                tail_start(t - 1)
            tile_body(t)
        tile_tail(NTILES - 1)

        final_waits = [("semO", P.count["semO"]), ("semY", P.count["semY"])] + \
                      [("semX%d" % k, P.count["semX%d" % k]) for k in range(8)]

        with nc.Block() as block:
            def emit(eng_name, e, extra_waits=()):
                for (waits, fn, semname, inc) in P.streams[eng_name]:
                    for (s, v) in waits:
                        e.wait_ge(sems[s], v)
                    ins = fn(e)
                    ins.then_inc(sems[semname], inc)
                for (s, v) in extra_waits:
                    e.wait_ge(sems[s], v)

            @block.tensor
            def _(e):
                emit("pe", e)

            @block.scalar
            def _(e):
                emit("act", e)

            @block.vector
            def _(e):
                emit("dve", e)

            @block.gpsimd
            def _(e):
                emit("pool", e)

            @block.sync
            def _(e):
                emit("sp", e, final_waits)
    return nc


def _pack_vec(inp):
    v = np.zeros((128, NV), np.float32)
    v[:, GM:GM + 8] = inp["g_mix"][0].reshape(8, 128).T
    v[:, GF:GF + 8] = inp["g_ffn"][0].reshape(8, 128).T
    v[:, GP:GP + 8] = inp["g_ple"][0].reshape(8, 128).T
    v[:, GFIN:GFIN + 8] = inp["g_final"].reshape(8, 128).T
    v[:, PSC:PSC + 4] = inp["pool_scale"][0].reshape(4, 128).T
    for k in range(3):
        v[:, CW + 4 * k:CW + 4 * k + 4] = inp["conv_w"][0][k].reshape(4, 128).T
        v[:, FW + 44 * k:FW + 44 * k + 44] = inp["ffn_conv_w"][0][k].reshape(44, 128).T
    v[:, CB:CB + 4] = inp["conv_b"][0].reshape(4, 128).T
    v[:, FB:FB + 44] = inp["ffn_conv_b"][0].reshape(44, 128).T
    return v


def kernel(**inp):
    inp = {k: np.asarray(v) for k, v in inp.items()}
    f = np.float32
    vec = _pack_vec(inp)
    w_in = inp["w_in"][0]
    w_up = inp["w_up"][0]
    shared = {
        "vec": vec,
        "w_in_u": np.ascontiguousarray(w_in[:, 0:512], f),
        "w_in_c": np.ascontiguousarray(np.stack([np.concatenate(
            [w_in[:, 512 + 128 * j:640 + 128 * j], w_in[:, 1024 + 128 * j:1152 + 128 * j],
             w_in[:, 1536 + 128 * j:1664 + 128 * j]], axis=1) for j in range(4)]), f),
        "w_pool": np.ascontiguousarray(inp["w_pool"][0], f),
        "w_out": np.ascontiguousarray(inp["w_out"][0], f),
        "w_up_u": np.ascontiguousarray(np.stack([np.concatenate(
            [w_up[:, 256 * i:256 * i + 256], w_up[:, 2816 + 256 * i:2816 + 256 * i + 256]], axis=1)
            for i in range(11)]), f),
        "w_down": np.ascontiguousarray(inp["w_down"][0], f),
        "w_gate": np.ascontiguousarray(inp["w_ple_gate"][0], f),
        "w_proj": np.ascontiguousarray(inp["w_ple_proj"][0], f),
    }
    in_maps = []
    for b in range(NCORES):
        sl = slice(NS * b, NS * b + NS)
        xT = np.concatenate([inp["x_prompt"][b].T, inp["x_sample"][sl, 0].T], axis=1)
        pT = np.concatenate([inp["p_prompt"][0, b].T, inp["p_sample"][0, sl, 0].T], axis=1)
        spv = inp["state_pool"][0, sl].reshape(NS, 15, 4, 128).transpose(3, 1, 2, 0)
        scv = inp["state_conv"][0, sl].reshape(NS, 2, 4, 128).transpose(3, 1, 2, 0)
        sfv = inp["state_ffn"][0, sl].reshape(NS, 2, 44, 128).transpose(3, 1, 2, 0)
        m = dict(shared)
        m.update({
            "xT": np.ascontiguousarray(xT, f), "pT": np.ascontiguousarray(pT, f),
            "sp_in": np.ascontiguousarray(spv, f), "sc_in": np.ascontiguousarray(scv, f),
            "sf_in": np.ascontiguousarray(sfv, f),
        })
        in_maps.append(m)

    nc = build_program()
    res = run_bass_kernel_spmd(nc, in_maps, core_ids=list(range(NCORES)))
    outs = res.results

    y_prompt = np.empty((8, NPROMPT, 1024), f)
    y_sample = np.empty((128, 1, 1024), f)
    npp = np.empty((1, 8, 15, 512), f)
    ncp = np.empty((1, 8, 2, 512), f)
    nfp = np.empty((1, 8, 2, 5632), f)
    nps = np.empty((1, 128, 15, 512), f)
    ncs = np.empty((1, 128, 2, 512), f)
    nfs = np.empty((1, 128, 2, 5632), f)
    for b in range(NCORES):
        o = outs[b]
        sl = slice(NS * b, NS * b + NS)
        yT = o["yT"]
        y_prompt[b] = yT[:, :NPROMPT].T
        y_sample[sl, 0] = yT[:, NPROMPT:].T
        npp[0, b] = o["npp"].transpose(2, 1, 0).reshape(15, 512)
        ncp[0, b] = o["ncp"].transpose(2, 1, 0).reshape(2, 512)
        nfp[0, b] = o["nfp"].transpose(2, 1, 0).reshape(2, 5632)
        nps[0, sl] = o["nps"].transpose(3, 1, 2, 0).reshape(NS, 15, 512)
        ncs[0, sl] = o["ncs"].transpose(3, 1, 2, 0).reshape(NS, 2, 512)
        nfs[0, sl] = o["nfs"].transpose(3, 1, 2, 0).reshape(NS, 2, 5632)
    return (y_prompt, y_sample, npp, ncp, nfp, nps, ncs, nfs)
```

```python
import numpy as np
from contextlib import ExitStack
import concourse.bass as bass
import concourse.mybir as mybir
from concourse.bass_utils import run_bass_kernel_spmd

F32 = mybir.dt.float32
F32R = mybir.dt.float32r
AF = mybir.ActivationFunctionType
ALU = mybir.AluOpType

NCORES = 8
NT = 688
NB = 344
HALO = 16
ROW = HALO + NT
NTILES = 3
NPROMPT = 2048
NS = 16
NCOLS = NPROMPT + NS
NSLOT = 3
SLOTW = 4096
NSCR = 10
EPS = 1e-6
FAST_RECIP = False

GM, GF, GP, GFIN, PSC, CW, CB, FW, FB, NV = 0, 8, 16, 24, 32, 36, 48, 52, 184, 228
POOL_W = (2, 4, 8, 16)


class Prog:
    ENGS = ("pe", "act", "dve", "pool", "sp")
    STRICT = ("pool", "dve", "act")

    def __init__(self):
        self.streams = {e: [] for e in self.ENGS}
        self.count = {}
        self.waited = {e: {} for e in self.ENGS}
        self.last_write = {}
        self.readers = {}

    def _deps(self, eng, r, w):
        deps = []
        for k in r:
            t = self.last_write.get(k)
            if t is not None:
                deps.append(t)
        strict = eng in self.STRICT
        for k in w:
            t = self.last_write.get(k)
            if t is not None and (strict or t[2] != eng):
                deps.append(t)
            for t2 in self.readers.get(k, ()):
                if strict or t2[2] != eng:
                    deps.append(t2)
        best = {}
        for (s, v, e) in deps:
            if eng == "pe" and e == "pe":
                continue
            if v > best.get(s, 0):
                best[s] = v
        waits = []
        for s, v in best.items():
            if self.waited[eng].get(s, 0) < v:
                self.waited[eng][s] = v
                waits.append((s, v))
        return waits

    def op(self, eng, fn, r=(), w=(), sem=None, inc=1, tok_eng=None):
        waits = self._deps(eng, r, w)
        semname = sem if sem is not None else "s_" + eng
        v = self.count.get(semname, 0) + inc
        self.count[semname] = v
        tok = (semname, v, tok_eng if tok_eng is not None else eng)
        self.streams[eng].append((waits, fn, semname, inc))
        for k in r:
            self.readers.setdefault(k, []).append(tok)
        for k in w:
            self.last_write[k] = tok
            self.readers[k] = []
        return tok

    def dma(self, queue, sem, fn, r=(), w=()):
        return self.op(queue, fn, r, w, sem=sem, inc=16, tok_eng="dma:" + sem)


def build_program():
    nc = bass.Bass("TRN2", target_bir_lowering=False)
    P = Prog()

    def din(name, shape):
        return nc.dram_tensor(name, shape, F32, kind="ExternalInput").ap()

    def dout(name, shape):
        return nc.dram_tensor(name, shape, F32, kind="ExternalOutput").ap()

    xT = din("xT", [1024, NCOLS])
    pT = din("pT", [256, NCOLS]).bitcast(F32R)
    vec_d = din("vec", [128, NV])
    sp_d = din("sp_in", [128, 15, 4, NS])
    sc_d = din("sc_in", [128, 2, 4, NS])
    sf_d = din("sf_in", [128, 2, 44, NS])
    w_inu_d = din("w_in_u", [1024, 512]).bitcast(F32R)
    w_inc_d = din("w_in_c", [4, 1024, 384]).bitcast(F32R)
    w_pool_d = din("w_pool", [4, 128, 128]).bitcast(F32R)
    w_out_d = din("w_out", [1024, 1024]).bitcast(F32R)
    w_upu_d = din("w_up_u", [11, 1024, 512]).bitcast(F32R)
    w_down_d = din("w_down", [2816, 1024]).bitcast(F32R)
    w_gate_d = din("w_gate", [1024, 1024]).bitcast(F32R)
    w_proj_d = din("w_proj", [256, 1024]).bitcast(F32R)

    yT = dout("yT", [1024, NCOLS])
    npp_d = dout("npp", [128, 4, 15])
    ncp_d = dout("ncp", [128, 4, 2])
    nfp_d = dout("nfp", [128, 44, 2])
    nps_d = dout("nps", [128, 15, 4, NS])
    ncs_d = dout("ncs", [128, 2, 4, NS])
    nfs_d = dout("nfs", [128, 2, 44, NS])

    es = ExitStack()
    with es:
        def sb(name, shape, dt=F32):
            return es.enter_context(nc.sbuf_tensor(name, shape, dt))

        RB = [sb("R0", [128, 8, NT]), sb("R1", [128, 8, NT])]
        H = sb("H", [128, 8, NT], F32R)
        GB = sb("GB", [128, 8, NT], F32R)
        PT = sb("PT", [128, 2, NT], F32R)
        SCR = sb("SCR", [128, NSCR, ROW])
        DROW = sb("DROW", [128, 2, NT], F32R)
        WS = sb("WS", [128, NSLOT, SLOTW], F32R)
        WP = sb("WP", [128, 4, 128], F32R)
        VEC = sb("VEC", [128, NV])
        ONESF = sb("ONESF", [128, 128])
        ONES = sb("ONES", [128, 128], F32R)
        EPSV = sb("EPSV", [128, 1])
        DUMMY = sb("DUMMY", [128, 2])
        SPs = sb("SPs", [128, 15, 4, NS])
        SCs = sb("SCs", [128, 2, 4, NS])
        SFs = sb("SFs", [128, 2, 44, NS])
        UPL = sb("UPL", [128, 44, 2])
        CHL = sb("CHL", [128, 4, 2])
        CHS = sb("CHS", [128, 4, NS])
        UL = sb("UL", [128, 4, 15])
        US = sb("US", [128, 4, NS])
        BASEC = sb("BASEC", [128, 4, NS])
        POOLS = sb("POOLS", [128, 4, NS])
        PACC = sb("PACC", [128, 4, NS])
        INVC = sb("INVC", [128, 4, 16])
        PS = es.enter_context(nc.psum_tensor("PS", [128, 8, 512], F32))
        BASES = SFs[:, 0, :, :]
        UPS = SFs[:, 1, :, :]

        sem_names = ["s_pe", "s_act", "s_dve", "s_pool", "semR0", "semR1", "semPT", "semC", "semWP",
                     "semY", "semO"] + ["semX%d" % k for k in range(8)] + \
                    ["semW%d_%d" % (i, q) for i in range(NSLOT) for q in range(3)]
        sems = {n: es.enter_context(nc.semaphore(n)) for n in sem_names}

        NROT = NSCR - 1
        RSTD1 = NSCR - 1

        def vcol(c):
            return VEC[:, c:c + 1]

        def blk2(ap):
            return ap.rearrange("p (b n) -> p b n", b=2)

        def psv(s):
            return PS[:, 2 * s:2 * s + 2, 0:NB]

        def scr(i, a=HALO, b=ROW):
            return SCR[:, i, a:b]

        rr = {"ps": 0, "scr": 0}
        reserved = set()

        def next_ps():
            while True:
                s = rr["ps"]
                rr["ps"] = (s + 1) % 4
                if s not in reserved:
                    return s

        def next_scr():
            s = rr["scr"]
            rr["scr"] = (s + 1) % NROT
            return s

        def kscr(i):
            return [("scr", i), ("scrh", i)]

        def src_cols(wd, a, b):
            return wd[:, a:b].rearrange("(k p) o -> p k o", p=128)

        units = []

        def add_unit(nk, ntot, parts):
            units.append((nk, ntot, parts))

        for t in range(NTILES):
            add_unit(8, 512, [(0, 512, src_cols(w_inu_d, 0, 512))])
            for j in range(4):
                add_unit(8, 384, [(0, 384, w_inc_d[j].rearrange("(k p) o -> p k o", p=128))])
            for ou in range(2):
                add_unit(8, 512, [(0, 512, src_cols(w_out_d, 512 * ou, 512 * ou + 512))])
            for (j0, j1) in ((0, 8), (8, 14), (14, 22)):
                for jj in range(j0, j1, 2):
                    add_unit(8, 512, [(0, 512, w_upu_d[jj // 2].rearrange("(k p) o -> p k o", p=128))])
                nk = j1 - j0
                for ou in range(2):
                    add_unit(nk, 512, [(0, 512, w_down_d[128 * j0:128 * j1, 512 * ou:512 * ou + 512]
                                        .rearrange("(k p) o -> p k o", p=128))])
            add_unit(8, 512, [(0, 512, src_cols(w_gate_d, 0, 512))])
            add_unit(2, 1024, [(0, 1024, src_cols(w_proj_d, 0, 1024))])
            add_unit(8, 512, [(0, 512, src_cols(w_gate_d, 512, 1024))])

        ustate = {"issued": 0, "next": 0, "done": set()}

        def uview(i):
            nk, ntot, _ = units[i]
            return WS[:, i % NSLOT, 0:nk * ntot].rearrange("p (k o) -> p k o", k=nk)

        def ukeys(i):
            return [("w", i % NSLOT, q) for q in range(3)]

        def issue_units(limit=None):
            while ustate["issued"] < len(units):
                if limit is not None and ustate["issued"] >= limit:
                    break
                i = ustate["issued"]
                if i >= NSLOT and (i - NSLOT) not in ustate["done"]:
                    break
                nk, ntot, parts = units[i]
                v = uview(i)
                slot = i % NSLOT
                for q, (off, ncol, src) in enumerate(parts):
                    if len(parts) == 1:
                        keys = ukeys(i)
                    elif q == len(parts) - 1:
                        keys = [("w", slot, qq) for qq in range(q, 3)]
                    else:
                        keys = [("w", slot, q)]
                    dst = v[:, :, off:off + ncol]
                    P.dma("pool", "semW%d_%d" % (slot, q),
                          (lambda e, dst=dst, src=src: e.dma_start(out=dst, in_=src)),
                          r=(), w=keys)
                ustate["issued"] += 1

        def take_unit():
            i = ustate["next"]
            ustate["next"] += 1
            assert i < ustate["issued"], "unit not issued (prefetch logic)"
            return i

        def unit_done(i):
            ustate["done"].add(i)
            issue_units()

        def mm_group(s, lhs_fn, rhs_fn, nk, rkeys):
            items = []
            for b in range(2):
                for k in range(nk):
                    items.append((PS[:, 2 * s + b, 0:NB], lhs_fn(k), rhs_fn(k, b), k == 0, k == nk - 1))

            def fn(e):
                ins = None
                for (o, l, r_, st, sp) in items:
                    ins = e.matmul(o, l, r_, start=st, stop=sp)
                return ins
            P.op("pe", fn, r=rkeys, w=[("ps", s)])

        def mm_kouter(slots, lhs_fns, rhs_fn, nk, rkeys_k, rkeys_common):
            for k in range(nk):
                items = []
                for ci, s in enumerate(slots):
                    for b in range(2):
                        items.append((PS[:, 2 * s + b, 0:NB], lhs_fns[ci](k), rhs_fn(k, b), k == 0, k == nk - 1))

                def fn(e, items=items):
                    ins = None
                    for (o, l, r_, st, sp) in items:
                        ins = e.matmul(o, l, r_, start=st, stop=sp)
                    return ins
                P.op("pe", fn, r=[rkeys_k(k)] + rkeys_common, w=[("ps", s) for s in slots])

        def act(out, in_, func, r, w, **kw):
            P.op("act", (lambda e: e.activation(out=out, in_=in_, func=func, **kw)), r=r, w=w)

        def tt(out, in0, in1, op, r, w):
            P.op("dve", (lambda e: e.tensor_tensor(out=out, in0=in0, in1=in1, op=op)), r=r, w=w)

        def stt(out, in0, scalar, in1, op0, op1, r, w):
            P.op("dve", (lambda e: e.scalar_tensor_tensor(out=out, in0=in0, scalar=scalar, in1=in1,
                                                          op0=op0, op1=op1)), r=r, w=w)

        def pcopy(out, in_, r, w):
            P.op("pool", (lambda e: e.tensor_copy(out=out, in_=in_)), r=r, w=w)

        P.dma("sp", "semC", lambda e: e.dma_start(out=VEC[:, :], in_=vec_d), w=["VEC"])
        for k in range(8):
            P.dma("sp", "semX%d" % k, (lambda e, k=k: e.dma_start(
                out=RB[0][:, k, :], in_=xT[128 * k:128 * k + 128, 0:NT])), w=[("R", 0, k)])
        P.op("pool", lambda e: e.memset(ONESF[:, :], 1.0), w=["ONESF"])
        P.op("pool", lambda e: e.memset(EPSV[:, :], EPS), w=["EPSV"])
        P.op("act", lambda e: e.activation(out=ONES[:, :], in_=ONESF[:, :], func=AF.Copy),
             r=["ONESF"], w=["ONES"])
        issue_units(limit=1)
        P.dma("pool", "semWP", lambda e: e.dma_start(out=WP[:, :, :],
                                                      in_=w_pool_d.rearrange("g c d -> c g d")), w=["WP"])
        P.op("pool", lambda e: e.memset(UPL[:, :, :], 0.0), w=[("UPL", c) for c in range(44)])
        P.op("pool", lambda e: e.memset(CHL[:, :, :], 0.0), w=[("CHL", c) for c in range(4)])
        P.op("pool", lambda e: e.memset(UL[:, :, :], 0.0), w=[("UL", c) for c in range(4)])
        for g, wdw in enumerate(POOL_W):
            for tcol in range(16):
                val = 1.0 / min(wdw, tcol + 1)
                P.op("pool", (lambda e, g=g, tcol=tcol, val=val: e.memset(INVC[:, g, tcol:tcol + 1], val)),
                     w=[("INVC", g)])
        issue_units()
        P.dma("pool", "semPT", lambda e: e.dma_start(
            out=PT[:, :, :], in_=pT[:, 0:NT].rearrange("(k p) n -> p k n", p=128)),
            w=[("PT", 0), ("PT", 1)])
        P.dma("sp", "semC", lambda e: e.dma_start(out=SPs[:, :, :, :], in_=sp_d), w=["SPs"])
        P.dma("sp", "semC", lambda e: e.dma_start(out=SCs[:, :, :, :], in_=sc_d), w=["SCs"])
        P.dma("sp", "semC", lambda e: e.dma_start(out=SFs[:, :, :, :], in_=sf_d), w=["SFs"])
        for kk in ("VEC", "SPs", "SCs", "SFs"):
            P.last_write[kk] = ("semC", P.count["semC"], "dma:semC")
        P.dma("sp", "semO", lambda e: e.dma_start(out=nfs_d[:, 0, :, :], in_=sf_d[:, 1, :, :]))

        def build_precompute_pieces():
            pieces = []

            def p_pools():
                act(PACC[:, :, :], SPs[:, 14, :, :], AF.Copy, r=["SPs"], w=["PACC"])
                act(POOLS[:, 0, :], PACC[:, 0, :], AF.Copy, r=["PACC"], w=[("POOLS", 0)])
            pieces.append(p_pools)
            row = 13
            for g in range(1, 4):
                lo = 15 - (POOL_W[g] - 1)
                rows = list(range(row, lo - 1, -1))
                row = lo - 1

                def p_g(g=g, rows=rows):
                    for rw in rows:
                        tt(PACC[:, :, :], PACC[:, :, :], SPs[:, rw, :, :], ALU.add, r=["PACC", "SPs"], w=["PACC"])
                    act(POOLS[:, g, :], PACC[:, g, :], AF.Copy, r=["PACC"], w=[("POOLS", g)])
                pieces.append(p_g)

            def p_c(j):
                act(BASEC[:, j, :], SCs[:, 0, j, :], AF.Identity, r=["SCs", "VEC"], w=[("CHS_base", j)],
                    scale=vcol(CW + j))
                stt(BASEC[:, j, :], SCs[:, 1, j, :], vcol(CW + 4 + j), BASEC[:, j, :], ALU.mult, ALU.add,
                    r=["SCs", "VEC", ("CHS_base", j)], w=[("CHS_base", j)])

            def p_s(c):
                act(BASES[:, c, :], SFs[:, 0, c, :], AF.Identity, r=["SFs", "VEC"], w=[("UPS_base", c)],
                    scale=vcol(FW + c))
                stt(BASES[:, c, :], SFs[:, 1, c, :], vcol(FW + 44 + c), BASES[:, c, :], ALU.mult, ALU.add,
                    r=["SFs", "VEC", ("UPS_base", c), ("UPS", c)], w=[("UPS_base", c)])
            for j in range(4):
                pieces.append(lambda j=j: p_c(j))
            for c in range(44):
                pieces.append(lambda c=c: p_s(c))
            return pieces

        pre_pieces = build_precompute_pieces()

        class NormAcc:
            def __init__(self, rb, sq_fn, maxpend=8):
                self.rb = rb
                self.maxpend = maxpend
                self.sq_fn = sq_fn
                self.s = next_ps()
                reserved.add(self.s)
                self.cnt = 0
                self.pending = []

            def square(self, k):
                while len(self.pending) >= self.maxpend:
                    self.mm()
                ap, key = self.sq_fn(k)
                act(ap, RB[self.rb][:, k, :], AF.Square, r=[("R", self.rb, k)], w=[key])
                self.pending.append(k)

            def mm(self):
                if not self.pending:
                    return
                k = self.pending.pop(0)
                ap, key = self.sq_fn(k)
                first, last = self.cnt == 0, self.cnt == 7
                self.cnt += 1
                s = self.s

                def fn(e):
                    ins = None
                    for b in range(2):
                        ins = e.matmul(PS[:, 2 * s + b, 0:NB], ONES[:, :], ap[:, b * NB:(b + 1) * NB],
                                       start=first, stop=last)
                    return ins
                P.op("pe", fn, r=[key, "ONES"], w=[("ps", s)])

            def finish(self, rs):
                while self.pending:
                    self.mm()
                assert self.cnt == 8
                sd = next_scr()
                act(blk2(scr(sd)), psv(self.s), AF.Ln, r=[("ps", self.s), "EPSV"], w=[("scr", sd)],
                    bias=EPSV[:, 0:1], scale=1.0 / 1024.0)
                reserved.discard(self.s)
                act(scr(rs), scr(sd), AF.Exp, r=[("scr", sd)], w=[("scr", rs)], scale=-0.5)
                return rs

        def preload_ln_table():
            act(DUMMY[:, 0:1], EPSV[:, 0:1], AF.Ln, r=["EPSV"], w=["DUMMY"])

        def preload_table(func):
            act(DUMMY[:, 1:2], EPSV[:, 0:1], func, r=["EPSV"], w=["DUMMY1"])

        def norm_apply(rb, gcol, rs, to_y=False):
            for k in range(8):
                out = RB[rb][:, k, :] if to_y else H[:, k, :]
                stt(out, RB[rb][:, k, :], vcol(gcol + k), scr(rs), ALU.mult, ALU.mult,
                    r=[("R", rb, k), ("scr", rs), "VEC"], w=[("R", rb, k) if to_y else ("H", k)])

        def sq_H(k):
            return H[:, k, :], ("H", k)

        def sq_GB(k):
            return GB[:, k, :], ("GB", k)

        def sq_D(k):
            return DROW[:, k % 2, :], ("DROW", k % 2)

        def conv_from_row(src, c, wcol, bcol, nch, base_t, key_l, L, key_s, S_, npr, has_s):
            pcopy(SCR[:, src, HALO - 2:HALO], L[:, c, :], r=[(key_l, c)], w=[("scrh", src)])
            pcopy(L[:, c, :], SCR[:, src, HALO + npr - 2:HALO + npr], r=[("scr", src)], w=[(key_l, c)])
            if has_s:
                pcopy(S_[:, c, :], SCR[:, src, HALO + npr:ROW], r=[("scr", src)], w=[(key_s, c)])
            acc = next_scr()
            act(scr(acc), scr(src), AF.Identity, r=[("scr", src), "VEC"], w=[("scr", acc)],
                scale=vcol(wcol + 2 * nch + c), bias=vcol(bcol + c))
            stt(SCR[:, acc, HALO:HALO + npr], SCR[:, src, HALO - 1:HALO + npr - 1], vcol(wcol + nch + c),
                SCR[:, acc, HALO:HALO + npr], ALU.mult, ALU.add,
                r=kscr(src) + [("scr", acc), "VEC"], w=[("scr", acc)])
            stt(SCR[:, acc, HALO:HALO + npr], SCR[:, src, HALO - 2:HALO + npr - 2], vcol(wcol + c),
                SCR[:, acc, HALO:HALO + npr], ALU.mult, ALU.add,
                r=kscr(src) + [("scr", acc), "VEC"], w=[("scr", acc)])
            if has_s:
                tt(SCR[:, acc, HALO + npr:ROW], SCR[:, acc, HALO + npr:ROW], base_t[:, c, :], ALU.add,
                   r=[("scr", acc), (key_s + "_base", c)], w=[("scr", acc)])
            return acc

        state = {"rs1": None, "n4": None, "ytodo": []}

        def proj_residual(rb, nk, src_key, src_fn, norm_make, hg_col=None):
            R = RB[rb]
            u0 = take_unit()
            W0 = uview(u0)
            u1 = None
            W1 = None
            skeys = [(src_key, k) for k in range(nk)]
            KO = 3
            sl = [next_ps() for _ in range(KO)]
            mm_kouter(sl, [(lambda k, q=q: W0[:, k, q * 128:(q + 1) * 128]) for q in range(KO)],
                      src_fn, nk, rkeys_k=lambda k: (src_key, k), rkeys_common=ukeys(u0))
            na = None
            for oc in range(8):
                q = oc % 4
                if oc == 4:
                    unit_done(u0)
                    u1 = take_unit()
                    W1 = uview(u1)
                if oc < KO:
                    s = sl[oc]
                else:
                    Wc_, uc_ = (W0, u0) if oc < 4 else (W1, u1)
                    s = next_ps()
                    mm_group(s, lambda k, q=q, Wc_=Wc_: Wc_[:, k, q * 128:(q + 1) * 128], src_fn, nk,
                             rkeys=skeys + ukeys(uc_))
                if na is not None and oc >= KO:
                    na.mm()
                    if len(na.pending) > 2:
                        na.mm()
                tt(blk2(R[:, oc, :]), psv(s), blk2(R[:, oc, :]), ALU.add,
                   r=[("ps", s), ("R", rb, oc)], w=[("R", rb, oc)])
                if na is None and norm_make is not None and oc >= 1:
                    na = norm_make()
                    for o2 in range(oc):
                        na.square(o2)
                if na is not None:
                    na.square(oc)
                if hg_col is not None:
                    act(H[:, oc, :], R[:, oc, :], AF.Identity, r=[("R", rb, oc), "VEC"], w=[("H", oc)],
                        scale=vcol(hg_col + oc))
                if pre_pieces and oc < 7:
                    pre_pieces.pop(0)()
            unit_done(u1)
            return na

        def emit_state_outputs():
            def out_dma(dst, src, r):
                P.dma("sp", "semO", (lambda e: e.dma_start(out=dst, in_=src)), r=r)

            out_dma(npp_d, UL[:, :, :], [("UL", g) for g in range(4)])
            out_dma(ncp_d, CHL[:, :, :], [("CHL", g) for g in range(4)])
            out_dma(nfp_d, UPL[:, :, :], [("UPL", c) for c in range(44)])
            out_dma(nps_d[:, 0:14, :, :], SPs[:, 1:15, :, :], ["SPs"])
            out_dma(nps_d[:, 14, :, :], US[:, :, :], [("US", g) for g in range(4)])
            out_dma(ncs_d[:, 0, :, :], SCs[:, 1, :, :], ["SCs"])
            out_dma(ncs_d[:, 1, :, :], CHS[:, :, :], [("CHS", g) for g in range(4)])
            out_dma(nfs_d[:, 1, :, :], UPS, [("UPS", c) for c in range(44)])

        def tile_head(t):
            rb = t % 2
            if t == 0:
                na = NormAcc(0, sq_H)
                for k in range(8):
                    na.square(k)
                    na.mm()
                na.finish(RSTD1)
            norm_apply(rb, GM, RSTD1)

        def tile_tail(t):
            rb = t % 2
            c0 = NT * t
            rs = next_scr()
            state["n4"].finish(rs)
            for k in range(8):
                stt(RB[rb][:, k, :], RB[rb][:, k, :], vcol(GFIN + k), scr(rs), ALU.mult, ALU.mult,
                    r=[("R", rb, k), ("scr", rs), "VEC"], w=[("R", rb, k)])
                P.dma("sp", "semX%d" % k, (lambda e, k=k: e.dma_start(
                    out=yT[128 * k:128 * k + 128, c0:c0 + NT], in_=RB[rb][:, k, :])),
                    r=[("R", rb, k)])

        def tail_start(t):
            state["n4"].finish(RSTD1)
            state["ytodo"] = [(t, k) for k in range(8)]

        def tail_piece(n):
            for _ in range(n):
                if not state["ytodo"]:
                    return
                t, k = state["ytodo"].pop(0)
                rb = t % 2
                stt(RB[rb][:, k, :], RB[rb][:, k, :], vcol(GFIN + k), scr(RSTD1), ALU.mult, ALU.mult,
                    r=[("R", rb, k), ("scr", RSTD1), "VEC"], w=[("R", rb, k)])
                if not state["ytodo"]:
                    c0 = NT * t
                    P.dma("sp", "semY", lambda e: e.dma_start(
                        out=yT[:, c0:c0 + NT].rearrange("(k p) n -> p k n", p=128), in_=RB[rb][:, :, :]),
                        r=[("R", rb, k) for k in range(8)])

        def tile_body(t):
            rb = t % 2
            R = RB[rb]
            has_s = (t == NTILES - 1)
            npr = NT - NS if has_s else NT
            hk = [("H", k) for k in range(8)]

            def rk(k):
                return ("R", rb, k)

            def hrhs(k, b):
                return H[:, k, b * NB:(b + 1) * NB]

            ui = take_unit()
            Wu = uview(ui)
            uslots = [next_ps() for _ in range(4)]
            mm_kouter(uslots, [(lambda k, g=g: Wu[:, k, g * 128:(g + 1) * 128]) for g in range(4)],
                      hrhs, 8, rkeys_k=lambda k: ("H", k), rkeys_common=ukeys(ui))
            unit_done(ui)
            for s_ in uslots:
                reserved.add(s_)

            def pool_chain(g):
                wdw = POOL_W[g]
                s = uslots[g]
                U = next_scr()
                act(blk2(scr(U)), psv(s), AF.Copy, r=[("ps", s)], w=[("scr", U)])
                reserved.discard(s)
                pcopy(SCR[:, U, 1:HALO], UL[:, g, :], r=[("UL", g)], w=[("scrh", U)])
                pcopy(UL[:, g, :], SCR[:, U, HALO + npr - 15:HALO + npr], r=[("scr", U)], w=[("UL", g)])
                if has_s:
                    pcopy(US[:, g, :], SCR[:, U, HALO + npr:ROW], r=[("scr", U)], w=[("US", g)])
                A = next_scr()
                B = next_scr()
                cur, sh, lo = U, 1, 2
                dst = A
                for lvl in range(g + 1):
                    tt(SCR[:, dst, lo:ROW], SCR[:, cur, lo:ROW], SCR[:, cur, lo - sh:ROW - sh], ALU.add,
                       r=kscr(cur), w=kscr(dst))
                    cur = dst
                    dst = B if cur == A else A
                    sh *= 2
                    lo *= 2
                Srow = cur
                di = g
                stt(GB[:, g, :], scr(Srow), 1.0 / wdw, scr(U), ALU.mult, ALU.subtract,
                    r=kscr(Srow) + kscr(U), w=[("GB", g)])
                tmp = dst
                if t == 0:
                    tt(SCR[:, tmp, 0:15], SCR[:, Srow, HALO:HALO + 15], INVC[:, g, 0:15], ALU.mult,
                       r=kscr(Srow) + [("INVC", g)], w=kscr(tmp))
                    tt(GB[:, g, 0:15], SCR[:, tmp, 0:15], SCR[:, U, HALO:HALO + 15], ALU.subtract,
                       r=kscr(tmp) + kscr(U) + [("GB", g)], w=[("GB", g)])
                if has_s:
                    tt(SCR[:, tmp, 0:NS], SCR[:, U, HALO + npr:ROW], POOLS[:, g, :], ALU.add,
                       r=kscr(U) + [("POOLS", g)], w=kscr(tmp))
                    stt(GB[:, g, npr:NT], SCR[:, tmp, 0:NS], 1.0 / wdw, SCR[:, U, HALO + npr:ROW],
                        ALU.mult, ALU.subtract, r=kscr(tmp) + kscr(U) + [("GB", g)], w=[("GB", g)])
                return di

            def pool_mm(g, di):
                s2 = next_ps()
                mm_group(s2, lambda k: WP[:, g, :], lambda k, b: GB[:, g, b * NB:(b + 1) * NB], 1,
                         rkeys=[("GB", g), "WP"])
                act(blk2(GB[:, g, :]), psv(s2), AF.Identity, r=[("ps", s2), "VEC"], w=[("GB", g)],
                    scale=vcol(PSC + g))

            def conv_unit(j):
                ci = take_unit()
                Wc = uview(ci)
                s_cg = next_ps()
                mm_group(s_cg, lambda k: Wc[:, k, 128:256], hrhs, 8, rkeys=hk + ukeys(ci))
                s_hv = next_ps()
                mm_group(s_hv, lambda k: Wc[:, k, 256:384], hrhs, 8, rkeys=hk + ukeys(ci))
                CG = next_scr()
                act(blk2(scr(CG)), psv(s_cg), AF.Copy, r=[("ps", s_cg)], w=[("scr", CG)])
                s_bg = next_ps()
                mm_group(s_bg, lambda k: Wc[:, k, 0:128], hrhs, 8, rkeys=hk + ukeys(ci))
                CH = next_scr()
                tt(blk2(scr(CH)), psv(s_hv), blk2(scr(CG)), ALU.mult, r=[("ps", s_hv), ("scr", CG)],
                   w=[("scr", CH)])
                acc = conv_from_row(CH, j, CW, CB, 4, BASEC, "CHL", CHL, "CHS", CHS, npr, has_s)
                tt(blk2(GB[:, 4 + j, :]), psv(s_bg), blk2(scr(acc)), ALU.mult,
                   r=[("ps", s_bg), ("scr", acc)], w=[("GB", 4 + j)])
                unit_done(ci)

            d0 = pool_chain(0)
            d1 = pool_chain(1)
            conv_unit(0)
            pool_mm(0, d0)
            pool_mm(1, d1)
            tail_piece(2)
            d2 = pool_chain(2)
            d3 = pool_chain(3)
            conv_unit(1)
            tail_piece(2)
            conv_unit(2)
            pool_mm(2, d2)
            pool_mm(3, d3)
            tail_piece(2)
            conv_unit(3)
            tail_piece(2)
            n2 = proj_residual(rb, 8, "GB", lambda k, b: GB[:, k, b * NB:(b + 1) * NB],
                               lambda: NormAcc(rb, sq_H))
            rs2 = next_scr()
            n2.finish(rs2)
            preload_table(AF.Silu)
            norm_apply(rb, GF, rs2)

            if t + 1 < NTILES:
                nb_ = (t + 1) % 2
                c1 = NT * (t + 1)
                P.dma("sp", "semR%d" % nb_, lambda e: e.dma_start(
                    out=RB[nb_][:, :, :], in_=xT[:, c1:c1 + NT].rearrange("(k p) n -> p k n", p=128)),
                    w=[("R", nb_, k) for k in range(8)])

            parts3 = ((0, 8), (8, 14), (14, 22))
            n1 = None
            n3 = None
            deferred = []

            def flush_deferred():
                while deferred:
                    deferred.pop(0)()

            for pi, (j0, j1) in enumerate(parts3):
                for jj in range(j0, j1, 2):
                    fi = take_unit()
                    Wf = uview(fi)
                    lhs4 = [(lambda k, q=q: Wf[:, k, q * 128:(q + 1) * 128]) for q in range(2)] + \
                           [(lambda k, q=q: Wf[:, k, 256 + q * 128:256 + (q + 1) * 128]) for q in range(2)]
                    if jj == 0:
                        sl2 = [next_ps() for _ in range(2)]
                        mm_kouter(sl2, [lhs4[0], lhs4[2]], hrhs, 8,
                                  rkeys_k=lambda k: ("H", k), rkeys_common=ukeys(fi))
                        pre = [(sl2[0], sl2[1])]
                    else:
                        pre = None
                    for q in range(2):
                        if pre is not None and q < len(pre):
                            s_a, s_v = pre[q]
                        else:
                            s_a = next_ps()
                            mm_group(s_a, lhs4[q], hrhs, 8, rkeys=hk + ukeys(fi))
                            s_v = next_ps()
                            mm_group(s_v, lhs4[2 + q], hrhs, 8, rkeys=hk + ukeys(fi))
                        j = jj + q
                        accs = []
                        for (s_x, c) in ((s_a, j), (s_v, 22 + j)):
                            UP = next_scr()
                            act(blk2(scr(UP)), psv(s_x), AF.Copy, r=[("ps", s_x)], w=[("scr", UP)])
                            accs.append(conv_from_row(UP, c, FW, FB, 44, BASES, "UPL", UPL, "UPS", UPS,
                                                      npr, has_s))
                        a_, v_ = accs
                        act(scr(a_), scr(a_), AF.Silu, r=[("scr", a_)], w=[("scr", a_)])
                        if j == 21 or (j == 13 and t + 1 < NTILES):
                            preload_ln_table()
                        tt(GB[:, j - j0, :], scr(a_), scr(v_), ALU.mult,
                           r=[("scr", a_), ("scr", v_)], w=[("GB", j - j0)])

                    unit_done(fi)
                flush_deferred()
                nk = j1 - j0
                mk = None
                if pi == 1 and t + 1 < NTILES:
                    mk = lambda: NormAcc((t + 1) % 2, sq_D, maxpend=2)
                if pi == 2:
                    mk = lambda: NormAcc(rb, sq_D, maxpend=2)
                na_ = proj_residual(rb, nk, "GB", lambda k, b: GB[:, k, b * NB:(b + 1) * NB], mk,
                                    hg_col=(GP if pi == 2 else None))
                if pi == 1:
                    n1 = na_
                if pi == 2:
                    n3 = na_
                if pi == 1 and n1 is not None:
                    n1.finish(RSTD1)
                    preload_table(AF.Silu)
            rs3 = next_scr()
            n3.finish(rs3)
            preload_table(AF.Sigmoid)
            if t == NTILES - 1:
                emit_state_outputs()
            assert not (t == 1 and pre_pieces), "precompute pieces left over"

            g0 = take_unit()
            pj = take_unit()
            g1 = take_unit()
            Wpj = uview(pj)
            Wg0 = uview(g0)
            Wg1 = uview(g1)
            ppk = [("PT", 0), ("PT", 1)] + ukeys(pj)
            KG = 3
            gs4 = [next_ps() for _ in range(KG)]
            mm_kouter(gs4, [(lambda k, q=q: Wg0[:, k, q * 128:(q + 1) * 128]) for q in range(KG)], hrhs, 8,
                      rkeys_k=lambda k: ("H", k), rkeys_common=ukeys(g0))
            gts = [next_scr() for _ in range(4)]
            for q in range(KG):
                GT = gts[q]
                tt(blk2(scr(GT)), psv(gs4[q]), blk2(scr(rs3)), ALU.mult, r=[("ps", gs4[q]), ("scr", rs3)],
                   w=[("scr", GT)])
                act(scr(GT), scr(GT), AF.Sigmoid, r=[("scr", GT)], w=[("scr", GT)])
            n4 = None
            for oc in range(8):
                if oc == 4:
                    unit_done(g0)
                if oc >= KG:
                    q = oc % 4
                    Wg_, ug_ = (Wg0, g0) if oc < 4 else (Wg1, g1)
                    sg = next_ps()
                    mm_group(sg, lambda k, q=q, Wg_=Wg_: Wg_[:, k, q * 128:(q + 1) * 128], hrhs, 8,
                             rkeys=hk + ukeys(ug_))
                sp_ = next_ps()
                mm_group(sp_, lambda k, oc=oc: Wpj[:, k, oc * 128:(oc + 1) * 128],
                         lambda k, b: PT[:, k, b * NB:(b + 1) * NB], 2, rkeys=ppk)
                if n4 is not None and oc >= 3:
                    n4.mm()
                GT = gts[oc % 4]
                if oc >= KG:
                    tt(blk2(scr(GT)), psv(sg), blk2(scr(rs3)), ALU.mult, r=[("ps", sg), ("scr", rs3)],
                       w=[("scr", GT)])
                    act(scr(GT), scr(GT), AF.Sigmoid, r=[("scr", GT)], w=[("scr", GT)])
                    if oc == 7:
                        preload_ln_table()
                tt(blk2(scr(GT)), psv(sp_), blk2(scr(GT)), ALU.mult, r=[("ps", sp_), ("scr", GT)],
                   w=[("scr", GT)])
                tt(R[:, oc, :], R[:, oc, :], scr(GT), ALU.add, r=[rk(oc), ("scr", GT)], w=[rk(oc)])
                if n4 is None and oc >= 1:
                    n4 = NormAcc(rb, sq_GB)
                    state["n4"] = n4
                    for o2 in range(oc):
                        n4.square(o2)
                if n4 is not None:
                    n4.square(oc)
            unit_done(pj)
            unit_done(g1)
            if t + 1 < NTILES:
                c1 = NT * (t + 1)
                P.dma("pool", "semPT", lambda e: e.dma_start(
                    out=PT[:, :, :], in_=pT[:, c1:c1 + NT].rearrange("(k p) n -> p k n", p=128)),
                    w=[("PT", 0), ("PT", 1)])

        for t in range(NTILES):
            tile_head(t)
            if t > 0:
                tail_start(t - 1)
            tile_body(t)
        tile_tail(NTILES - 1)

        final_waits = [("semO", P.count["semO"]), ("semY", P.count["semY"])] + \
                      [("semX%d" % k, P.count["semX%d" % k]) for k in range(8)]

        with nc.Block() as block:
            def emit(eng_name, e, extra_waits=()):
                for (waits, fn, semname, inc) in P.streams[eng_name]:
                    for (s, v) in waits:
                        e.wait_ge(sems[s], v)
                    ins = fn(e)
                    ins.then_inc(sems[semname], inc)
                for (s, v) in extra_waits:
                    e.wait_ge(sems[s], v)

            @block.tensor
            def _(e):
                emit("pe", e)

            @block.scalar
            def _(e):
                emit("act", e)

            @block.vector
            def _(e):
                emit("dve", e)

            @block.gpsimd
            def _(e):
                emit("pool", e)

            @block.sync
            def _(e):
                emit("sp", e, final_waits)
    return nc


def _pack_vec(inp):
    v = np.zeros((128, NV), np.float32)
    v[:, GM:GM + 8] = inp["g_mix"][0].reshape(8, 128).T
    v[:, GF:GF + 8] = inp["g_ffn"][0].reshape(8, 128).T
    v[:, GP:GP + 8] = inp["g_ple"][0].reshape(8, 128).T
    v[:, GFIN:GFIN + 8] = inp["g_final"].reshape(8, 128).T
    v[:, PSC:PSC + 4] = inp["pool_scale"][0].reshape(4, 128).T
    for k in range(3):
        v[:, CW + 4 * k:CW + 4 * k + 4] = inp["conv_w"][0][k].reshape(4, 128).T
        v[:, FW + 44 * k:FW + 44 * k + 44] = inp["ffn_conv_w"][0][k].reshape(44, 128).T
    v[:, CB:CB + 4] = inp["conv_b"][0].reshape(4, 128).T
    v[:, FB:FB + 44] = inp["ffn_conv_b"][0].reshape(44, 128).T
    return v


def kernel(**inp):
    inp = {k: np.asarray(v) for k, v in inp.items()}
    f = np.float32
    vec = _pack_vec(inp)
    w_in = inp["w_in"][0]
    w_up = inp["w_up"][0]
    shared = {
        "vec": vec,
        "w_in_u": np.ascontiguousarray(w_in[:, 0:512], f),
        "w_in_c": np.ascontiguousarray(np.stack([np.concatenate(
            [w_in[:, 512 + 128 * j:640 + 128 * j], w_in[:, 1024 + 128 * j:1152 + 128 * j],
             w_in[:, 1536 + 128 * j:1664 + 128 * j]], axis=1) for j in range(4)]), f),
        "w_pool": np.ascontiguousarray(inp["w_pool"][0], f),
        "w_out": np.ascontiguousarray(inp["w_out"][0], f),
        "w_up_u": np.ascontiguousarray(np.stack([np.concatenate(
            [w_up[:, 256 * i:256 * i + 256], w_up[:, 2816 + 256 * i:2816 + 256 * i + 256]], axis=1)
            for i in range(11)]), f),
        "w_down": np.ascontiguousarray(inp["w_down"][0], f),
        "w_gate": np.ascontiguousarray(inp["w_ple_gate"][0], f),
        "w_proj": np.ascontiguousarray(inp["w_ple_proj"][0], f),
    }
    in_maps = []
    for b in range(NCORES):
        sl = slice(NS * b, NS * b + NS)
        xT = np.concatenate([inp["x_prompt"][b].T, inp["x_sample"][sl, 0].T], axis=1)
        pT = np.concatenate([inp["p_prompt"][0, b].T, inp["p_sample"][0, sl, 0].T], axis=1)
        spv = inp["state_pool"][0, sl].reshape(NS, 15, 4, 128).transpose(3, 1, 2, 0)
        scv = inp["state_conv"][0, sl].reshape(NS, 2, 4, 128).transpose(3, 1, 2, 0)
        sfv = inp["state_ffn"][0, sl].reshape(NS, 2, 44, 128).transpose(3, 1, 2, 0)
        m = dict(shared)
        m.update({
            "xT": np.ascontiguousarray(xT, f), "pT": np.ascontiguousarray(pT, f),
            "sp_in": np.ascontiguousarray(spv, f), "sc_in": np.ascontiguousarray(scv, f),
            "sf_in": np.ascontiguousarray(sfv, f),
        })
        in_maps.append(m)

    nc = build_program()
    res = run_bass_kernel_spmd(nc, in_maps, core_ids=list(range(NCORES)))
    outs = res.results

    y_prompt = np.empty((8, NPROMPT, 1024), f)
    y_sample = np.empty((128, 1, 1024), f)
    npp = np.empty((1, 8, 15, 512), f)
    ncp = np.empty((1, 8, 2, 512), f)
    nfp = np.empty((1, 8, 2, 5632), f)
    nps = np.empty((1, 128, 15, 512), f)
    ncs = np.empty((1, 128, 2, 512), f)
    nfs = np.empty((1, 128, 2, 5632), f)
    for b in range(NCORES):
        o = outs[b]
        sl = slice(NS * b, NS * b + NS)
        yT = o["yT"]
        y_prompt[b] = yT[:, :NPROMPT].T
        y_sample[sl, 0] = yT[:, NPROMPT:].T
        npp[0, b] = o["npp"].transpose(2, 1, 0).reshape(15, 512)
        ncp[0, b] = o["ncp"].transpose(2, 1, 0).reshape(2, 512)
        nfp[0, b] = o["nfp"].transpose(2, 1, 0).reshape(2, 5632)
        nps[0, sl] = o["nps"].transpose(3, 1, 2, 0).reshape(NS, 15, 512)
        ncs[0, sl] = o["ncs"].transpose(3, 1, 2, 0).reshape(NS, 2, 512)
        nfs[0, sl] = o["nfs"].transpose(3, 1, 2, 0).reshape(NS, 2, 5632)
    return (y_prompt, y_sample, npp, ncp, nfp, nps, ncs, nfs)
```

```python
import numpy as np
from contextlib import ExitStack
import concourse.bass as bass
import concourse.mybir as mybir
from concourse.bass_utils import run_bass_kernel_spmd

F32 = mybir.dt.float32
F32R = mybir.dt.float32r
AF = mybir.ActivationFunctionType
ALU = mybir.AluOpType

NCORES = 8
NT = 688
NB = 344
HALO = 16
ROW = HALO + NT
NTILES = 3
NPROMPT = 2048
NS = 16
NCOLS = NPROMPT + NS
NSLOT = 3
SLOTW = 4096
NSCR = 10
EPS = 1e-6
FAST_RECIP = False

GM, GF, GP, GFIN, PSC, CW, CB, FW, FB, NV = 0, 8, 16, 24, 32, 36, 48, 52, 184, 228
POOL_W = (2, 4, 8, 16)


PARTS3 = ((0, 8), (8, 14), (14, 22))


def ffn_sequence():
    seq = []
    for pi, (j0, j1) in enumerate(PARTS3):
        ups = list(range(j0, j1, 2))
        for jj in ups[(1 if pi > 0 else 0):]:
            seq.append(("up", pi, jj, False))
        if pi + 1 < len(PARTS3):
            seq.append(("up", pi + 1, PARTS3[pi + 1][0], True))
        seq.append(("down", pi))
    return seq


class Prog:
    ENGS = ("pe", "act", "dve", "pool", "sp")
    STRICT = ("pool", "dve", "act")

    def __init__(self):
        self.streams = {e: [] for e in self.ENGS}
        self.count = {}
        self.waited = {e: {} for e in self.ENGS}
        self.last_write = {}
        self.readers = {}

    def _deps(self, eng, r, w):
        deps = []
        for k in r:
            t = self.last_write.get(k)
            if t is not None:
                deps.append(t)
        strict = eng in self.STRICT
        for k in w:
            t = self.last_write.get(k)
            if t is not None and (strict or t[2] != eng):
                deps.append(t)
            for t2 in self.readers.get(k, ()):
                if strict or t2[2] != eng:
                    deps.append(t2)
        best = {}
        for (s, v, e) in deps:
            if eng == "pe" and e == "pe":
                continue
            if v > best.get(s, 0):
                best[s] = v
        waits = []
        for s, v in best.items():
            if self.waited[eng].get(s, 0) < v:
                self.waited[eng][s] = v
                waits.append((s, v))
        return waits

    def op(self, eng, fn, r=(), w=(), sem=None, inc=1, tok_eng=None):
        waits = self._deps(eng, r, w)
        semname = sem if sem is not None else "s_" + eng
        v = self.count.get(semname, 0) + inc
        self.count[semname] = v
        tok = (semname, v, tok_eng if tok_eng is not None else eng)
        self.streams[eng].append((waits, fn, semname, inc))
        for k in r:
            self.readers.setdefault(k, []).append(tok)
        for k in w:
            self.last_write[k] = tok
            self.readers[k] = []
        return tok

    def dma(self, queue, sem, fn, r=(), w=()):
        return self.op(queue, fn, r, w, sem=sem, inc=16, tok_eng="dma:" + sem)


def build_program():
    nc = bass.Bass("TRN2", target_bir_lowering=False)
    P = Prog()

    def din(name, shape):
        return nc.dram_tensor(name, shape, F32, kind="ExternalInput").ap()

    def dout(name, shape):
        return nc.dram_tensor(name, shape, F32, kind="ExternalOutput").ap()

    xT = din("xT", [1024, NCOLS])
    pT = din("pT", [256, NCOLS]).bitcast(F32R)
    vec_d = din("vec", [128, NV])
    sp_d = din("sp_in", [128, 15, 4, NS])
    sc_d = din("sc_in", [128, 2, 4, NS])
    sf_d = din("sf_in", [128, 2, 44, NS])
    w_inu_d = din("w_in_u", [1024, 512]).bitcast(F32R)
    w_inc_d = din("w_in_c", [4, 1024, 384]).bitcast(F32R)
    w_pool_d = din("w_pool", [4, 128, 128]).bitcast(F32R)
    w_out_d = din("w_out", [1024, 1024]).bitcast(F32R)
    w_upu_d = din("w_up_u", [11, 1024, 512]).bitcast(F32R)
    w_down_d = din("w_down", [2816, 1024]).bitcast(F32R)
    w_gate_d = din("w_gate", [1024, 1024]).bitcast(F32R)
    w_proj_d = din("w_proj", [256, 1024]).bitcast(F32R)

    yT = dout("yT", [1024, NCOLS])
    npp_d = dout("npp", [128, 4, 15])
    ncp_d = dout("ncp", [128, 4, 2])
    nfp_d = dout("nfp", [128, 44, 2])
    nps_d = dout("nps", [128, 15, 4, NS])
    ncs_d = dout("ncs", [128, 2, 4, NS])
    nfs_d = dout("nfs", [128, 2, 44, NS])

    es = ExitStack()
    with es:
        def sb(name, shape, dt=F32):
            return es.enter_context(nc.sbuf_tensor(name, shape, dt))

        RB = [sb("R0", [128, 8, NT]), sb("R1", [128, 8, NT])]
        H = sb("H", [128, 8, NT], F32R)
        GB = sb("GB", [128, 8, NT], F32R)
        PT = sb("PT", [128, 2, NT], F32R)
        SCR = sb("SCR", [128, NSCR, ROW])
        DROW = sb("DROW", [128, 2, NT], F32R)
        WS = sb("WS", [128, NSLOT, SLOTW], F32R)
        WP = sb("WP", [128, 4, 128], F32R)
        VEC = sb("VEC", [128, NV])
        ONESF = sb("ONESF", [128, 128])
        ONES = sb("ONES", [128, 128], F32R)
        EPSV = sb("EPSV", [128, 1])
        DUMMY = sb("DUMMY", [128, 2])
        SPs = sb("SPs", [128, 15, 4, NS])
        SCs = sb("SCs", [128, 2, 4, NS])
        SFs = sb("SFs", [128, 2, 44, NS])
        UPL = sb("UPL", [128, 44, 2])
        CHL = sb("CHL", [128, 4, 2])
        CHS = sb("CHS", [128, 4, NS])
        UL = sb("UL", [128, 4, 15])
        US = sb("US", [128, 4, NS])
        BASEC = sb("BASEC", [128, 4, NS])
        POOLS = sb("POOLS", [128, 4, NS])
        PACC = sb("PACC", [128, 4, NS])
        INVC = sb("INVC", [128, 4, 16])
        PS = es.enter_context(nc.psum_tensor("PS", [128, 8, 512], F32))
        BASES = SFs[:, 0, :, :]
        UPS = SFs[:, 1, :, :]

        sem_names = ["s_pe", "s_act", "s_dve", "s_pool", "semR0", "semR1", "semPT", "semC", "semWP",
                     "semY", "semO"] + ["semX%d" % k for k in range(8)] + \
                    ["semW%d_%d" % (i, q) for i in range(NSLOT) for q in range(3)]
        sems = {n: es.enter_context(nc.semaphore(n)) for n in sem_names}

        NROT = NSCR - 1
        RSTD1 = NSCR - 1

        def vcol(c):
            return VEC[:, c:c + 1]

        def blk2(ap):
            return ap.rearrange("p (b n) -> p b n", b=2)

        def psv(s):
            return PS[:, 2 * s:2 * s + 2, 0:NB]

        def scr(i, a=HALO, b=ROW):
            return SCR[:, i, a:b]

        rr = {"ps": 0, "scr": 0}
        reserved = set()

        def next_ps():
            while True:
                s = rr["ps"]
                rr["ps"] = (s + 1) % 4
                if s not in reserved:
                    return s

        def next_scr():
            s = rr["scr"]
            rr["scr"] = (s + 1) % NROT
            return s

        def kscr(i):
            return [("scr", i), ("scrh", i)]

        def src_cols(wd, a, b):
            return wd[:, a:b].rearrange("(k p) o -> p k o", p=128)

        units = []

        def add_unit(nk, ntot, parts):
            units.append((nk, ntot, parts))

        for t in range(NTILES):
            add_unit(8, 512, [(0, 512, src_cols(w_inu_d, 0, 512))])
            for j in range(4):
                add_unit(8, 384, [(0, 384, w_inc_d[j].rearrange("(k p) o -> p k o", p=128))])
            for ou in range(2):
                add_unit(8, 512, [(0, 512, src_cols(w_out_d, 512 * ou, 512 * ou + 512))])
            for item in ffn_sequence():
                if item[0] == "up":
                    jj = item[2]
                    add_unit(8, 512, [(0, 512, w_upu_d[jj // 2].rearrange("(k p) o -> p k o", p=128))])
                else:
                    j0, j1 = PARTS3[item[1]]
                    nk = j1 - j0
                    for ou in range(2):
                        add_unit(nk, 512, [(0, 512, w_down_d[128 * j0:128 * j1, 512 * ou:512 * ou + 512]
                                            .rearrange("(k p) o -> p k o", p=128))])
            add_unit(8, 512, [(0, 512, src_cols(w_gate_d, 0, 512))])
            add_unit(2, 1024, [(0, 1024, src_cols(w_proj_d, 0, 1024))])
            add_unit(8, 512, [(0, 512, src_cols(w_gate_d, 512, 1024))])

        ustate = {"issued": 0, "next": 0, "done": set()}

        def uview(i):
            nk, ntot, _ = units[i]
            return WS[:, i % NSLOT, 0:nk * ntot].rearrange("p (k o) -> p k o", k=nk)

        def ukeys(i):
            return [("w", i % NSLOT, q) for q in range(3)]

        def issue_units(limit=None):
            while ustate["issued"] < len(units):
                if limit is not None and ustate["issued"] >= limit:
                    break
                i = ustate["issued"]
                if i >= NSLOT and (i - NSLOT) not in ustate["done"]:
                    break
                nk, ntot, parts = units[i]
                v = uview(i)
                slot = i % NSLOT
                for q, (off, ncol, src) in enumerate(parts):
                    if len(parts) == 1:
                        keys = ukeys(i)
                    elif q == len(parts) - 1:
                        keys = [("w", slot, qq) for qq in range(q, 3)]
                    else:
                        keys = [("w", slot, q)]
                    dst = v[:, :, off:off + ncol]
                    P.dma("pool", "semW%d_%d" % (slot, q),
                          (lambda e, dst=dst, src=src: e.dma_start(out=dst, in_=src)),
                          r=(), w=keys)
                ustate["issued"] += 1

        def take_unit():
            i = ustate["next"]
            ustate["next"] += 1
            assert i < ustate["issued"], "unit not issued (prefetch logic)"
            return i

        def unit_done(i):
            ustate["done"].add(i)
            issue_units()

        def mm_group(s, lhs_fn, rhs_fn, nk, rkeys):
            items = []
            for b in range(2):
                for k in range(nk):
                    items.append((PS[:, 2 * s + b, 0:NB], lhs_fn(k), rhs_fn(k, b), k == 0, k == nk - 1))

            def fn(e):
                ins = None
                for (o, l, r_, st, sp) in items:
                    ins = e.matmul(o, l, r_, start=st, stop=sp)
                return ins
            P.op("pe", fn, r=rkeys, w=[("ps", s)])

        def mm_kouter(slots, lhs_fns, rhs_fn, nk, rkeys_k, rkeys_common):
            for k in range(nk):
                items = []
                for ci, s in enumerate(slots):
                    for b in range(2):
                        items.append((PS[:, 2 * s + b, 0:NB], lhs_fns[ci](k), rhs_fn(k, b), k == 0, k == nk - 1))

                def fn(e, items=items):
                    ins = None
                    for (o, l, r_, st, sp) in items:
                        ins = e.matmul(o, l, r_, start=st, stop=sp)
                    return ins
                P.op("pe", fn, r=[rkeys_k(k)] + rkeys_common, w=[("ps", s) for s in slots])

        def act(out, in_, func, r, w, **kw):
            P.op("act", (lambda e: e.activation(out=out, in_=in_, func=func, **kw)), r=r, w=w)

        def tt(out, in0, in1, op, r, w):
            P.op("dve", (lambda e: e.tensor_tensor(out=out, in0=in0, in1=in1, op=op)), r=r, w=w)

        def stt(out, in0, scalar, in1, op0, op1, r, w):
            P.op("dve", (lambda e: e.scalar_tensor_tensor(out=out, in0=in0, scalar=scalar, in1=in1,
                                                          op0=op0, op1=op1)), r=r, w=w)

        def pcopy(out, in_, r, w):
            P.op("pool", (lambda e: e.tensor_copy(out=out, in_=in_)), r=r, w=w)

        P.dma("sp", "semC", lambda e: e.dma_start(out=VEC[:, :], in_=vec_d), w=["VEC"])
        for k in range(8):
            P.dma("sp", "semX%d" % k, (lambda e, k=k: e.dma_start(
                out=RB[0][:, k, :], in_=xT[128 * k:128 * k + 128, 0:NT])), w=[("R", 0, k)])
        P.op("pool", lambda e: e.memset(ONESF[:, :], 1.0), w=["ONESF"])
        P.op("pool", lambda e: e.memset(EPSV[:, :], EPS), w=["EPSV"])
        P.op("act", lambda e: e.activation(out=ONES[:, :], in_=ONESF[:, :], func=AF.Copy),
             r=["ONESF"], w=["ONES"])
        issue_units(limit=1)
        P.dma("pool", "semWP", lambda e: e.dma_start(out=WP[:, :, :],
                                                      in_=w_pool_d.rearrange("g c d -> c g d")), w=["WP"])
        P.op("pool", lambda e: e.memset(UPL[:, :, :], 0.0), w=[("UPL", c) for c in range(44)])
        P.op("pool", lambda e: e.memset(CHL[:, :, :], 0.0), w=[("CHL", c) for c in range(4)])
        P.op("pool", lambda e: e.memset(UL[:, :, :], 0.0), w=[("UL", c) for c in range(4)])
        for g, wdw in enumerate(POOL_W):
            for tcol in range(16):
                val = 1.0 / min(wdw, tcol + 1)
                P.op("pool", (lambda e, g=g, tcol=tcol, val=val: e.memset(INVC[:, g, tcol:tcol + 1], val)),
                     w=[("INVC", g)])
        issue_units()
        P.dma("pool", "semPT", lambda e: e.dma_start(
            out=PT[:, :, :], in_=pT[:, 0:NT].rearrange("(k p) n -> p k n", p=128)),
            w=[("PT", 0), ("PT", 1)])
        P.dma("sp", "semC", lambda e: e.dma_start(out=SPs[:, :, :, :], in_=sp_d), w=["SPs"])
        P.dma("sp", "semC", lambda e: e.dma_start(out=SCs[:, :, :, :], in_=sc_d), w=["SCs"])
        P.dma("sp", "semC", lambda e: e.dma_start(out=SFs[:, :, :, :], in_=sf_d), w=["SFs"])
        for kk in ("VEC", "SPs", "SCs", "SFs"):
            P.last_write[kk] = ("semC", P.count["semC"], "dma:semC")
        P.dma("sp", "semO", lambda e: e.dma_start(out=nfs_d[:, 0, :, :], in_=sf_d[:, 1, :, :]))

        def build_precompute_pieces():
            pieces = []

            def p_pools():
                act(PACC[:, :, :], SPs[:, 14, :, :], AF.Copy, r=["SPs"], w=["PACC"])
                act(POOLS[:, 0, :], PACC[:, 0, :], AF.Copy, r=["PACC"], w=[("POOLS", 0)])
            pieces.append(p_pools)
            row = 13
            for g in range(1, 4):
                lo = 15 - (POOL_W[g] - 1)
                rows = list(range(row, lo - 1, -1))
                row = lo - 1

                def p_g(g=g, rows=rows):
                    for rw in rows:
                        tt(PACC[:, :, :], PACC[:, :, :], SPs[:, rw, :, :], ALU.add, r=["PACC", "SPs"], w=["PACC"])
                    act(POOLS[:, g, :], PACC[:, g, :], AF.Copy, r=["PACC"], w=[("POOLS", g)])
                pieces.append(p_g)

            def p_c(j):
                act(BASEC[:, j, :], SCs[:, 0, j, :], AF.Identity, r=["SCs", "VEC"], w=[("CHS_base", j)],
                    scale=vcol(CW + j))
                stt(BASEC[:, j, :], SCs[:, 1, j, :], vcol(CW + 4 + j), BASEC[:, j, :], ALU.mult, ALU.add,
                    r=["SCs", "VEC", ("CHS_base", j)], w=[("CHS_base", j)])

            def p_s(c):
                act(BASES[:, c, :], SFs[:, 0, c, :], AF.Identity, r=["SFs", "VEC"], w=[("UPS_base", c)],
                    scale=vcol(FW + c))
                stt(BASES[:, c, :], SFs[:, 1, c, :], vcol(FW + 44 + c), BASES[:, c, :], ALU.mult, ALU.add,
                    r=["SFs", "VEC", ("UPS_base", c), ("UPS", c)], w=[("UPS_base", c)])
            for j in range(4):
                pieces.append(lambda j=j: p_c(j))
            for c in range(44):
                pieces.append(lambda c=c: p_s(c))
            return pieces

        pre_pieces = build_precompute_pieces()

        class NormAcc:
            def __init__(self, rb, sq_fn, maxpend=8):
                self.rb = rb
                self.maxpend = maxpend
                self.sq_fn = sq_fn
                self.s = next_ps()
                reserved.add(self.s)
                self.cnt = 0
                self.pending = []

            def square(self, k):
                while len(self.pending) >= self.maxpend:
                    self.mm()
                ap, key = self.sq_fn(k)
                act(ap, RB[self.rb][:, k, :], AF.Square, r=[("R", self.rb, k)], w=[key])
                self.pending.append(k)

            def mm(self):
                if not self.pending:
                    return
                k = self.pending.pop(0)
                ap, key = self.sq_fn(k)
                first, last = self.cnt == 0, self.cnt == 7
                self.cnt += 1
                s = self.s

                def fn(e):
                    ins = None
                    for b in range(2):
                        ins = e.matmul(PS[:, 2 * s + b, 0:NB], ONES[:, :], ap[:, b * NB:(b + 1) * NB],
                                       start=first, stop=last)
                    return ins
                P.op("pe", fn, r=[key, "ONES"], w=[("ps", s)])

            def finish(self, rs):
                while self.pending:
                    self.mm()
                assert self.cnt == 8
                sd = next_scr()
                act(blk2(scr(sd)), psv(self.s), AF.Ln, r=[("ps", self.s), "EPSV"], w=[("scr", sd)],
                    bias=EPSV[:, 0:1], scale=1.0 / 1024.0)
                reserved.discard(self.s)
                act(scr(rs), scr(sd), AF.Exp, r=[("scr", sd)], w=[("scr", rs)], scale=-0.5)
                return rs

        def preload_ln_table():
            act(DUMMY[:, 0:1], EPSV[:, 0:1], AF.Ln, r=["EPSV"], w=["DUMMY"])

        def preload_table(func):
            act(DUMMY[:, 1:2], EPSV[:, 0:1], func, r=["EPSV"], w=["DUMMY1"])

        def norm_apply(rb, gcol, rs, to_y=False):
            for k in range(8):
                out = RB[rb][:, k, :] if to_y else H[:, k, :]
                stt(out, RB[rb][:, k, :], vcol(gcol + k), scr(rs), ALU.mult, ALU.mult,
                    r=[("R", rb, k), ("scr", rs), "VEC"], w=[("R", rb, k) if to_y else ("H", k)])

        def sq_H(k):
            return H[:, k, :], ("H", k)

        def sq_GB(k):
            return GB[:, k, :], ("GB", k)

        def sq_D(k):
            return DROW[:, k % 2, :], ("DROW", k % 2)

        def conv_from_row(src, c, wcol, bcol, nch, base_t, key_l, L, key_s, S_, npr, has_s):
            pcopy(SCR[:, src, HALO - 2:HALO], L[:, c, :], r=[(key_l, c)], w=[("scrh", src)])
            pcopy(L[:, c, :], SCR[:, src, HALO + npr - 2:HALO + npr], r=[("scr", src)], w=[(key_l, c)])
            if has_s:
                pcopy(S_[:, c, :], SCR[:, src, HALO + npr:ROW], r=[("scr", src)], w=[(key_s, c)])
            acc = next_scr()
            act(scr(acc), scr(src), AF.Identity, r=[("scr", src), "VEC"], w=[("scr", acc)],
                scale=vcol(wcol + 2 * nch + c), bias=vcol(bcol + c))
            stt(SCR[:, acc, HALO:HALO + npr], SCR[:, src, HALO - 1:HALO + npr - 1], vcol(wcol + nch + c),
                SCR[:, acc, HALO:HALO + npr], ALU.mult, ALU.add,
                r=kscr(src) + [("scr", acc), "VEC"], w=[("scr", acc)])
            stt(SCR[:, acc, HALO:HALO + npr], SCR[:, src, HALO - 2:HALO + npr - 2], vcol(wcol + c),
                SCR[:, acc, HALO:HALO + npr], ALU.mult, ALU.add,
                r=kscr(src) + [("scr", acc), "VEC"], w=[("scr", acc)])
            if has_s:
                tt(SCR[:, acc, HALO + npr:ROW], SCR[:, acc, HALO + npr:ROW], base_t[:, c, :], ALU.add,
                   r=[("scr", acc), (key_s + "_base", c)], w=[("scr", acc)])
            return acc

        state = {"rs1": None, "n4": None, "ytodo": []}

        def proj_residual(rb, nk, src_key, src_fn, norm_make, hg_col=None):
            R = RB[rb]
            u0 = take_unit()
            W0 = uview(u0)
            u1 = None
            W1 = None
            skeys = [(src_key, k) for k in range(nk)]
            KO = 3
            sl = [next_ps() for _ in range(KO)]
            mm_kouter(sl, [(lambda k, q=q: W0[:, k, q * 128:(q + 1) * 128]) for q in range(KO)],
                      src_fn, nk, rkeys_k=lambda k: (src_key, k), rkeys_common=ukeys(u0))
            na = None
            for oc in range(8):
                q = oc % 4
                if oc == 4:
                    unit_done(u0)
                    u1 = take_unit()
                    W1 = uview(u1)
                if oc < KO:
                    s = sl[oc]
                else:
                    Wc_, uc_ = (W0, u0) if oc < 4 else (W1, u1)
                    s = next_ps()
                    mm_group(s, lambda k, q=q, Wc_=Wc_: Wc_[:, k, q * 128:(q + 1) * 128], src_fn, nk,
                             rkeys=skeys + ukeys(uc_))
                if na is not None and oc >= KO:
                    na.mm()
                    if len(na.pending) > 2:
                        na.mm()
                tt(blk2(R[:, oc, :]), psv(s), blk2(R[:, oc, :]), ALU.add,
                   r=[("ps", s), ("R", rb, oc)], w=[("R", rb, oc)])
                if na is None and norm_make is not None and oc >= 1:
                    na = norm_make()
                    for o2 in range(oc):
                        na.square(o2)
                if na is not None:
                    na.square(oc)
                if hg_col is not None:
                    act(H[:, oc, :], R[:, oc, :], AF.Identity, r=[("R", rb, oc), "VEC"], w=[("H", oc)],
                        scale=vcol(hg_col + oc))
                if pre_pieces and oc < 7:
                    pre_pieces.pop(0)()
            unit_done(u1)
            return na

        def emit_state_outputs():
            def out_dma(dst, src, r):
                P.dma("sp", "semO", (lambda e: e.dma_start(out=dst, in_=src)), r=r)

            out_dma(npp_d, UL[:, :, :], [("UL", g) for g in range(4)])
            out_dma(ncp_d, CHL[:, :, :], [("CHL", g) for g in range(4)])
            out_dma(nfp_d, UPL[:, :, :], [("UPL", c) for c in range(44)])
            out_dma(nps_d[:, 0:14, :, :], SPs[:, 1:15, :, :], ["SPs"])
            out_dma(nps_d[:, 14, :, :], US[:, :, :], [("US", g) for g in range(4)])
            out_dma(ncs_d[:, 0, :, :], SCs[:, 1, :, :], ["SCs"])
            out_dma(ncs_d[:, 1, :, :], CHS[:, :, :], [("CHS", g) for g in range(4)])
            out_dma(nfs_d[:, 1, :, :], UPS, [("UPS", c) for c in range(44)])

        def tile_head(t):
            rb = t % 2
            if t == 0:
                na = NormAcc(0, sq_H)
                for k in range(8):
                    na.square(k)
                    na.mm()
                na.finish(RSTD1)
            norm_apply(rb, GM, RSTD1)

        def tile_tail(t):
            rb = t % 2
            c0 = NT * t
            rs = next_scr()
            state["n4"].finish(rs)
            for k in range(8):
                stt(RB[rb][:, k, :], RB[rb][:, k, :], vcol(GFIN + k), scr(rs), ALU.mult, ALU.mult,
                    r=[("R", rb, k), ("scr", rs), "VEC"], w=[("R", rb, k)])
                P.dma("sp", "semX%d" % k, (lambda e, k=k: e.dma_start(
                    out=yT[128 * k:128 * k + 128, c0:c0 + NT], in_=RB[rb][:, k, :])),
                    r=[("R", rb, k)])

        def tail_start(t):
            state["n4"].finish(RSTD1)
            state["ytodo"] = [(t, k) for k in range(8)]

        def tail_piece(n):
            for _ in range(n):
                if not state["ytodo"]:
                    return
                t, k = state["ytodo"].pop(0)
                rb = t % 2
                stt(RB[rb][:, k, :], RB[rb][:, k, :], vcol(GFIN + k), scr(RSTD1), ALU.mult, ALU.mult,
                    r=[("R", rb, k), ("scr", RSTD1), "VEC"], w=[("R", rb, k)])
                if not state["ytodo"]:
                    c0 = NT * t
                    P.dma("sp", "semY", lambda e: e.dma_start(
                        out=yT[:, c0:c0 + NT].rearrange("(k p) n -> p k n", p=128), in_=RB[rb][:, :, :]),
                        r=[("R", rb, k) for k in range(8)])

        def tile_body(t):
            rb = t % 2
            R = RB[rb]
            has_s = (t == NTILES - 1)
            npr = NT - NS if has_s else NT
            hk = [("H", k) for k in range(8)]

            def rk(k):
                return ("R", rb, k)

            def hrhs(k, b):
                return H[:, k, b * NB:(b + 1) * NB]

            ui = take_unit()
            Wu = uview(ui)
            uslots = [next_ps() for _ in range(4)]
            mm_kouter(uslots, [(lambda k, g=g: Wu[:, k, g * 128:(g + 1) * 128]) for g in range(4)],
                      hrhs, 8, rkeys_k=lambda k: ("H", k), rkeys_common=ukeys(ui))
            unit_done(ui)
            for s_ in uslots:
                reserved.add(s_)

            def pool_chain(g):
                wdw = POOL_W[g]
                s = uslots[g]
                U = next_scr()
                act(blk2(scr(U)), psv(s), AF.Copy, r=[("ps", s)], w=[("scr", U)])
                reserved.discard(s)
                pcopy(SCR[:, U, 1:HALO], UL[:, g, :], r=[("UL", g)], w=[("scrh", U)])
                pcopy(UL[:, g, :], SCR[:, U, HALO + npr - 15:HALO + npr], r=[("scr", U)], w=[("UL", g)])
                if has_s:
                    pcopy(US[:, g, :], SCR[:, U, HALO + npr:ROW], r=[("scr", U)], w=[("US", g)])
                A = next_scr()
                B = next_scr()
                cur, sh, lo = U, 1, 2
                dst = A
                for lvl in range(g + 1):
                    tt(SCR[:, dst, lo:ROW], SCR[:, cur, lo:ROW], SCR[:, cur, lo - sh:ROW - sh], ALU.add,
                       r=kscr(cur), w=kscr(dst))
                    cur = dst
                    dst = B if cur == A else A
                    sh *= 2
                    lo *= 2
                Srow = cur
                di = g
                stt(GB[:, g, :], scr(Srow), 1.0 / wdw, scr(U), ALU.mult, ALU.subtract,
                    r=kscr(Srow) + kscr(U), w=[("GB", g)])
                tmp = dst
                if t == 0:
                    tt(SCR[:, tmp, 0:15], SCR[:, Srow, HALO:HALO + 15], INVC[:, g, 0:15], ALU.mult,
                       r=kscr(Srow) + [("INVC", g)], w=kscr(tmp))
                    tt(GB[:, g, 0:15], SCR[:, tmp, 0:15], SCR[:, U, HALO:HALO + 15], ALU.subtract,
                       r=kscr(tmp) + kscr(U) + [("GB", g)], w=[("GB", g)])
                if has_s:
                    tt(SCR[:, tmp, 0:NS], SCR[:, U, HALO + npr:ROW], POOLS[:, g, :], ALU.add,
                       r=kscr(U) + [("POOLS", g)], w=kscr(tmp))
                    stt(GB[:, g, npr:NT], SCR[:, tmp, 0:NS], 1.0 / wdw, SCR[:, U, HALO + npr:ROW],
                        ALU.mult, ALU.subtract, r=kscr(tmp) + kscr(U) + [("GB", g)], w=[("GB", g)])
                return di

            def pool_mm(g, di):
                s2 = next_ps()
                mm_group(s2, lambda k: WP[:, g, :], lambda k, b: GB[:, g, b * NB:(b + 1) * NB], 1,
                         rkeys=[("GB", g), "WP"])
                act(blk2(GB[:, g, :]), psv(s2), AF.Identity, r=[("ps", s2), "VEC"], w=[("GB", g)],
                    scale=vcol(PSC + g))

            def conv_unit(j):
                ci = take_unit()
                Wc = uview(ci)
                s_cg = next_ps()
                mm_group(s_cg, lambda k: Wc[:, k, 128:256], hrhs, 8, rkeys=hk + ukeys(ci))
                s_hv = next_ps()
                mm_group(s_hv, lambda k: Wc[:, k, 256:384], hrhs, 8, rkeys=hk + ukeys(ci))
                CG = next_scr()
                act(blk2(scr(CG)), psv(s_cg), AF.Copy, r=[("ps", s_cg)], w=[("scr", CG)])
                s_bg = next_ps()
                mm_group(s_bg, lambda k: Wc[:, k, 0:128], hrhs, 8, rkeys=hk + ukeys(ci))
                CH = next_scr()
                tt(blk2(scr(CH)), psv(s_hv), blk2(scr(CG)), ALU.mult, r=[("ps", s_hv), ("scr", CG)],
                   w=[("scr", CH)])
                acc = conv_from_row(CH, j, CW, CB, 4, BASEC, "CHL", CHL, "CHS", CHS, npr, has_s)
                tt(blk2(GB[:, 4 + j, :]), psv(s_bg), blk2(scr(acc)), ALU.mult,
                   r=[("ps", s_bg), ("scr", acc)], w=[("GB", 4 + j)])
                unit_done(ci)

            d0 = pool_chain(0)
            d1 = pool_chain(1)
            conv_unit(0)
            pool_mm(0, d0)
            pool_mm(1, d1)
            tail_piece(2)
            d2 = pool_chain(2)
            d3 = pool_chain(3)
            conv_unit(1)
            tail_piece(2)
            conv_unit(2)
            pool_mm(2, d2)
            pool_mm(3, d3)
            tail_piece(2)
            conv_unit(3)
            tail_piece(2)
            n2 = proj_residual(rb, 8, "GB", lambda k, b: GB[:, k, b * NB:(b + 1) * NB],
                               lambda: NormAcc(rb, sq_H))
            rs2 = next_scr()
            n2.finish(rs2)
            preload_table(AF.Silu)
            norm_apply(rb, GF, rs2)

            if t + 1 < NTILES:
                nb_ = (t + 1) % 2
                c1 = NT * (t + 1)
                P.dma("sp", "semR%d" % nb_, lambda e: e.dma_start(
                    out=RB[nb_][:, :, :], in_=xT[:, c1:c1 + NT].rearrange("(k p) n -> p k n", p=128)),
                    w=[("R", nb_, k) for k in range(8)])

            parts3 = PARTS3
            n1 = None
            n3 = None
            deferred = []

            def flush_deferred():
                while deferred:
                    deferred.pop(0)()

            def up_unit(pi, jj, pulled):
                j0 = parts3[pi][0]
                fi = take_unit()
                Wf = uview(fi)
                lhs4 = [(lambda k, q=q: Wf[:, k, q * 128:(q + 1) * 128]) for q in range(2)] + \
                       [(lambda k, q=q: Wf[:, k, 256 + q * 128:256 + (q + 1) * 128]) for q in range(2)]
                if jj == 0:
                    sl2 = [next_ps() for _ in range(2)]
                    mm_kouter(sl2, [lhs4[0], lhs4[2]], hrhs, 8,
                              rkeys_k=lambda k: ("H", k), rkeys_common=ukeys(fi))
                    pre = [(sl2[0], sl2[1])]
                else:
                    pre = None
                for q in range(2):
                    if pre is not None and q < len(pre):
                        s_a, s_v = pre[q]
                    else:
                        s_a = next_ps()
                        mm_group(s_a, lhs4[q], hrhs, 8, rkeys=hk + ukeys(fi))
                        s_v = next_ps()
                        mm_group(s_v, lhs4[2 + q], hrhs, 8, rkeys=hk + ukeys(fi))
                    j = jj + q
                    accs = []
                    for (s_x, c) in ((s_a, j), (s_v, 22 + j)):
                        UP = next_scr()
                        act(blk2(scr(UP)), psv(s_x), AF.Copy, r=[("ps", s_x)], w=[("scr", UP)])
                        accs.append(conv_from_row(UP, c, FW, FB, 44, BASES, "UPL", UPL, "UPS", UPS,
                                                  npr, has_s))
                    a_, v_ = accs
                    act(scr(a_), scr(a_), AF.Silu, r=[("scr", a_)], w=[("scr", a_)])
                    if j == 21 or (j == 15 and t + 1 < NTILES):
                        preload_ln_table()

                    def gate_mul(a_=a_, v_=v_, jo=j - j0):
                        tt(GB[:, jo, :], scr(a_), scr(v_), ALU.mult,
                           r=[("scr", a_), ("scr", v_)], w=[("GB", jo)])
                    if pulled:
                        deferred.append(gate_mul)
                    else:
                        gate_mul()
                unit_done(fi)

            for item in ffn_sequence():
                if item[0] == "up":
                    up_unit(item[1], item[2], item[3])
                    continue
                pi = item[1]
                j0, j1 = parts3[pi]
                nk = j1 - j0
                mk = None
                if pi == 1 and t + 1 < NTILES:
                    mk = lambda: NormAcc((t + 1) % 2, sq_D, maxpend=2)
                if pi == 2:
                    mk = lambda: NormAcc(rb, sq_D, maxpend=2)
                na_ = proj_residual(rb, nk, "GB", lambda k, b: GB[:, k, b * NB:(b + 1) * NB], mk,
                                    hg_col=(GP if pi == 2 else None))
                if pi == 1:
                    n1 = na_
                if pi == 2:
                    n3 = na_
                if pi == 1 and n1 is not None:
                    n1.finish(RSTD1)
                    preload_table(AF.Silu)
                flush_deferred()
            rs3 = next_scr()
            n3.finish(rs3)
            preload_table(AF.Sigmoid)
            if t == NTILES - 1:
                emit_state_outputs()
            assert not (t == 1 and pre_pieces), "precompute pieces left over"

            g0 = take_unit()
            pj = take_unit()
            g1 = take_unit()
            Wpj = uview(pj)
            Wg0 = uview(g0)
            Wg1 = uview(g1)
            ppk = [("PT", 0), ("PT", 1)] + ukeys(pj)
            KG = 3
            gs4 = [next_ps() for _ in range(KG)]
            mm_kouter(gs4, [(lambda k, q=q: Wg0[:, k, q * 128:(q + 1) * 128]) for q in range(KG)], hrhs, 8,
                      rkeys_k=lambda k: ("H", k), rkeys_common=ukeys(g0))
            gts = [next_scr() for _ in range(4)]
            for q in range(KG):
                GT = gts[q]
                tt(blk2(scr(GT)), psv(gs4[q]), blk2(scr(rs3)), ALU.mult, r=[("ps", gs4[q]), ("scr", rs3)],
                   w=[("scr", GT)])
                act(scr(GT), scr(GT), AF.Sigmoid, r=[("scr", GT)], w=[("scr", GT)])
            n4 = None
            for oc in range(8):
                if oc == 4:
                    unit_done(g0)
                if oc >= KG:
                    q = oc % 4
                    Wg_, ug_ = (Wg0, g0) if oc < 4 else (Wg1, g1)
                    sg = next_ps()
                    mm_group(sg, lambda k, q=q, Wg_=Wg_: Wg_[:, k, q * 128:(q + 1) * 128], hrhs, 8,
                             rkeys=hk + ukeys(ug_))
                sp_ = next_ps()
                mm_group(sp_, lambda k, oc=oc: Wpj[:, k, oc * 128:(oc + 1) * 128],
                         lambda k, b: PT[:, k, b * NB:(b + 1) * NB], 2, rkeys=ppk)
                if n4 is not None and oc >= 3:
                    n4.mm()
                GT = gts[oc % 4]
                if oc >= KG:
                    tt(blk2(scr(GT)), psv(sg), blk2(scr(rs3)), ALU.mult, r=[("ps", sg), ("scr", rs3)],
                       w=[("scr", GT)])
                    act(scr(GT), scr(GT), AF.Sigmoid, r=[("scr", GT)], w=[("scr", GT)])
                    if oc == 7:
                        preload_ln_table()
                tt(blk2(scr(GT)), psv(sp_), blk2(scr(GT)), ALU.mult, r=[("ps", sp_), ("scr", GT)],
                   w=[("scr", GT)])
                tt(R[:, oc, :], R[:, oc, :], scr(GT), ALU.add, r=[rk(oc), ("scr", GT)], w=[rk(oc)])
                if n4 is None and oc >= 1:
                    n4 = NormAcc(rb, sq_GB)
                    state["n4"] = n4
                    for o2 in range(oc):
                        n4.square(o2)
                if n4 is not None:
                    n4.square(oc)
            unit_done(pj)
            unit_done(g1)
            if t + 1 < NTILES:
                c1 = NT * (t + 1)
                P.dma("pool", "semPT", lambda e: e.dma_start(
                    out=PT[:, :, :], in_=pT[:, c1:c1 + NT].rearrange("(k p) n -> p k n", p=128)),
                    w=[("PT", 0), ("PT", 1)])

        for t in range(NTILES):
            tile_head(t)
            if t > 0:
                tail_start(t - 1)
            tile_body(t)
        tile_tail(NTILES - 1)

        final_waits = [("semO", P.count["semO"]), ("semY", P.count["semY"])] + \
                      [("semX%d" % k, P.count["semX%d" % k]) for k in range(8)]

        with nc.Block() as block:
            def emit(eng_name, e, extra_waits=()):
                for (waits, fn, semname, inc) in P.streams[eng_name]:
                    for (s, v) in waits:
                        e.wait_ge(sems[s], v)
                    ins = fn(e)
                    ins.then_inc(sems[semname], inc)
                for (s, v) in extra_waits:
                    e.wait_ge(sems[s], v)

            @block.tensor
            def _(e):
                emit("pe", e)

            @block.scalar
            def _(e):
                emit("act", e)

            @block.vector
            def _(e):
                emit("dve", e)

            @block.gpsimd
            def _(e):
                emit("pool", e)

            @block.sync
            def _(e):
                emit("sp", e, final_waits)
    return nc


def _pack_vec(inp):
    v = np.zeros((128, NV), np.float32)
    v[:, GM:GM + 8] = inp["g_mix"][0].reshape(8, 128).T
    v[:, GF:GF + 8] = inp["g_ffn"][0].reshape(8, 128).T
    v[:, GP:GP + 8] = inp["g_ple"][0].reshape(8, 128).T
    v[:, GFIN:GFIN + 8] = inp["g_final"].reshape(8, 128).T
    v[:, PSC:PSC + 4] = inp["pool_scale"][0].reshape(4, 128).T
    for k in range(3):
        v[:, CW + 4 * k:CW + 4 * k + 4] = inp["conv_w"][0][k].reshape(4, 128).T
        v[:, FW + 44 * k:FW + 44 * k + 44] = inp["ffn_conv_w"][0][k].reshape(44, 128).T
    v[:, CB:CB + 4] = inp["conv_b"][0].reshape(4, 128).T
    v[:, FB:FB + 44] = inp["ffn_conv_b"][0].reshape(44, 128).T
    return v


def kernel(**inp):
    inp = {k: np.asarray(v) for k, v in inp.items()}
    f = np.float32
    vec = _pack_vec(inp)
    w_in = inp["w_in"][0]
    w_up = inp["w_up"][0]
    shared = {
        "vec": vec,
        "w_in_u": np.ascontiguousarray(w_in[:, 0:512], f),
        "w_in_c": np.ascontiguousarray(np.stack([np.concatenate(
            [w_in[:, 512 + 128 * j:640 + 128 * j], w_in[:, 1024 + 128 * j:1152 + 128 * j],
             w_in[:, 1536 + 128 * j:1664 + 128 * j]], axis=1) for j in range(4)]), f),
        "w_pool": np.ascontiguousarray(inp["w_pool"][0], f),
        "w_out": np.ascontiguousarray(inp["w_out"][0], f),
        "w_up_u": np.ascontiguousarray(np.stack([np.concatenate(
            [w_up[:, 256 * i:256 * i + 256], w_up[:, 2816 + 256 * i:2816 + 256 * i + 256]], axis=1)
            for i in range(11)]), f),
        "w_down": np.ascontiguousarray(inp["w_down"][0], f),
        "w_gate": np.ascontiguousarray(inp["w_ple_gate"][0], f),
        "w_proj": np.ascontiguousarray(inp["w_ple_proj"][0], f),
    }
    in_maps = []
    for b in range(NCORES):
        sl = slice(NS * b, NS * b + NS)
        xT = np.concatenate([inp["x_prompt"][b].T, inp["x_sample"][sl, 0].T], axis=1)
        pT = np.concatenate([inp["p_prompt"][0, b].T, inp["p_sample"][0, sl, 0].T], axis=1)
        spv = inp["state_pool"][0, sl].reshape(NS, 15, 4, 128).transpose(3, 1, 2, 0)
        scv = inp["state_conv"][0, sl].reshape(NS, 2, 4, 128).transpose(3, 1, 2, 0)
        sfv = inp["state_ffn"][0, sl].reshape(NS, 2, 44, 128).transpose(3, 1, 2, 0)
        m = dict(shared)
        m.update({
            "xT": np.ascontiguousarray(xT, f), "pT": np.ascontiguousarray(pT, f),
            "sp_in": np.ascontiguousarray(spv, f), "sc_in": np.ascontiguousarray(scv, f),
            "sf_in": np.ascontiguousarray(sfv, f),
        })
        in_maps.append(m)

    nc = build_program()
    res = run_bass_kernel_spmd(nc, in_maps, core_ids=list(range(NCORES)))
    outs = res.results

    y_prompt = np.empty((8, NPROMPT, 1024), f)
    y_sample = np.empty((128, 1, 1024), f)
    npp = np.empty((1, 8, 15, 512), f)
    ncp = np.empty((1, 8, 2, 512), f)
    nfp = np.empty((1, 8, 2, 5632), f)
    nps = np.empty((1, 128, 15, 512), f)
    ncs = np.empty((1, 128, 2, 512), f)
    nfs = np.empty((1, 128, 2, 5632), f)
    for b in range(NCORES):
        o = outs[b]
        sl = slice(NS * b, NS * b + NS)
        yT = o["yT"]
        y_prompt[b] = yT[:, :NPROMPT].T
        y_sample[sl, 0] = yT[:, NPROMPT:].T
        npp[0, b] = o["npp"].transpose(2, 1, 0).reshape(15, 512)
        ncp[0, b] = o["ncp"].transpose(2, 1, 0).reshape(2, 512)
        nfp[0, b] = o["nfp"].transpose(2, 1, 0).reshape(2, 5632)
        nps[0, sl] = o["nps"].transpose(3, 1, 2, 0).reshape(NS, 15, 512)
        ncs[0, sl] = o["ncs"].transpose(3, 1, 2, 0).reshape(NS, 2, 512)
        nfs[0, sl] = o["nfs"].transpose(3, 1, 2, 0).reshape(NS, 2, 5632)
    return (y_prompt, y_sample, npp, ncp, nfp, nps, ncs, nfs)
```

```python
import numpy as np
from contextlib import ExitStack
import concourse.bass as bass
import concourse.mybir as mybir
from concourse.bass_utils import run_bass_kernel_spmd

F32 = mybir.dt.float32
F32R = mybir.dt.float32r
AF = mybir.ActivationFunctionType
ALU = mybir.AluOpType

NCORES = 8
NT = 688
NB = 344
HALO = 16
ROW = HALO + NT
NTILES = 3
NPROMPT = 2048
NS = 16
NCOLS = NPROMPT + NS
NSLOT = 3
SLOTW = 4096
NSCR = 10
EPS = 1e-6
FAST_RECIP = False

GM, GF, GP, GFIN, PSC, CW, CB, FW, FB, NV = 0, 8, 16, 24, 32, 36, 48, 52, 184, 228
POOL_W = (2, 4, 8, 16)


PARTS3 = ((0, 8), (8, 14), (14, 22))


def ffn_sequence():
    seq = []
    for pi, (j0, j1) in enumerate(PARTS3):
        ups = list(range(j0, j1, 2))
        for jj in ups[(1 if pi > 0 else 0):]:
            seq.append(("up", pi, jj, False))
        if pi + 1 < len(PARTS3):
            seq.append(("up", pi + 1, PARTS3[pi + 1][0], True))
        seq.append(("down", pi))
    return seq


class Prog:
    ENGS = ("pe", "act", "dve", "pool", "sp")
    STRICT = ("pool", "dve", "act")

    def __init__(self):
        self.streams = {e: [] for e in self.ENGS}
        self.count = {}
        self.waited = {e: {} for e in self.ENGS}
        self.last_write = {}
        self.readers = {}

    def _deps(self, eng, r, w):
        deps = []
        for k in r:
            t = self.last_write.get(k)
            if t is not None:
                deps.append(t)
        strict = eng in self.STRICT
        for k in w:
            t = self.last_write.get(k)
            if t is not None and (strict or t[2] != eng):
                deps.append(t)
            for t2 in self.readers.get(k, ()):
                if strict or t2[2] != eng:
                    deps.append(t2)
        best = {}
        for (s, v, e) in deps:
            if eng == "pe" and e == "pe":
                continue
            if v > best.get(s, 0):
                best[s] = v
        waits = []
        for s, v in best.items():
            if self.waited[eng].get(s, 0) < v:
                self.waited[eng][s] = v
                waits.append((s, v))
        return waits

    def op(self, eng, fn, r=(), w=(), sem=None, inc=1, tok_eng=None):
        waits = self._deps(eng, r, w)
        semname = sem if sem is not None else "s_" + eng
        v = self.count.get(semname, 0) + inc
        self.count[semname] = v
        tok = (semname, v, tok_eng if tok_eng is not None else eng)
        self.streams[eng].append((waits, fn, semname, inc))
        for k in r:
            self.readers.setdefault(k, []).append(tok)
        for k in w:
            self.last_write[k] = tok
            self.readers[k] = []
        return tok

    def dma(self, queue, sem, fn, r=(), w=()):
        return self.op(queue, fn, r, w, sem=sem, inc=16, tok_eng="dma:" + sem)


def build_program():
    nc = bass.Bass("TRN2", target_bir_lowering=False)
    P = Prog()

    def din(name, shape):
        return nc.dram_tensor(name, shape, F32, kind="ExternalInput").ap()

    def dout(name, shape):
        return nc.dram_tensor(name, shape, F32, kind="ExternalOutput").ap()

    xT = din("xT", [1024, NCOLS])
    pT = din("pT", [256, NCOLS]).bitcast(F32R)
    vec_d = din("vec", [128, NV])
    sp_d = din("sp_in", [128, 15, 4, NS])
    sc_d = din("sc_in", [128, 2, 4, NS])
    sf_d = din("sf_in", [128, 2, 44, NS])
    w_inu_d = din("w_in_u", [1024, 512]).bitcast(F32R)
    w_inc_d = din("w_in_c", [4, 1024, 384]).bitcast(F32R)
    w_pool_d = din("w_pool", [4, 128, 128]).bitcast(F32R)
    w_out_d = din("w_out", [1024, 1024]).bitcast(F32R)
    w_upu_d = din("w_up_u", [11, 1024, 512]).bitcast(F32R)
    w_down_d = din("w_down", [2816, 1024]).bitcast(F32R)
    w_gate_d = din("w_gate", [1024, 1024]).bitcast(F32R)
    w_proj_d = din("w_proj", [256, 1024]).bitcast(F32R)

    yT = dout("yT", [1024, NCOLS])
    npp_d = dout("npp", [128, 4, 15])
    ncp_d = dout("ncp", [128, 4, 2])
    nfp_d = dout("nfp", [128, 44, 2])
    nps_d = dout("nps", [128, 15, 4, NS])
    ncs_d = dout("ncs", [128, 2, 4, NS])
    nfs_d = dout("nfs", [128, 2, 44, NS])

    es = ExitStack()
    with es:
        def sb(name, shape, dt=F32):
            return es.enter_context(nc.sbuf_tensor(name, shape, dt))

        RB = [sb("R0", [128, 8, NT]), sb("R1", [128, 8, NT])]
        H = sb("H", [128, 8, NT], F32R)
        GB = sb("GB", [128, 8, NT], F32R)
        PT = sb("PT", [128, 2, NT], F32R)
        SCR = sb("SCR", [128, NSCR, ROW])
        DROW = sb("DROW", [128, 2, NT], F32R)
        WS = sb("WS", [128, NSLOT, SLOTW], F32R)
        WP = sb("WP", [128, 4, 128], F32R)
        VEC = sb("VEC", [128, NV])
        ONESF = sb("ONESF", [128, 128])
        ONES = sb("ONES", [128, 128], F32R)
        EPSV = sb("EPSV", [128, 1])
        DUMMY = sb("DUMMY", [128, 2])
        SPs = sb("SPs", [128, 15, 4, NS])
        SCs = sb("SCs", [128, 2, 4, NS])
        SFs = sb("SFs", [128, 2, 44, NS])
        UPL = sb("UPL", [128, 44, 2])
        CHL = sb("CHL", [128, 4, 2])
        CHS = sb("CHS", [128, 4, NS])
        UL = sb("UL", [128, 4, 15])
        US = sb("US", [128, 4, NS])
        BASEC = sb("BASEC", [128, 4, NS])
        POOLS = sb("POOLS", [128, 4, NS])
        PACC = sb("PACC", [128, 4, NS])
        INVC = sb("INVC", [128, 4, 16])
        PS = es.enter_context(nc.psum_tensor("PS", [128, 8, 512], F32))
        BASES = SFs[:, 0, :, :]
        UPS = SFs[:, 1, :, :]

        sem_names = ["s_pe", "s_act", "s_dve", "s_pool", "semR0", "semR1", "semPT", "semC", "semWP",
                     "semY", "semO"] + ["semX%d" % k for k in range(8)] + \
                    ["semW%d_%d" % (i, q) for i in range(NSLOT) for q in range(3)]
        sems = {n: es.enter_context(nc.semaphore(n)) for n in sem_names}

        NROT = NSCR - 1
        RSTD1 = NSCR - 1

        def vcol(c):
            return VEC[:, c:c + 1]

        def blk2(ap):
            return ap.rearrange("p (b n) -> p b n", b=2)

        def psv(s):
            return PS[:, 2 * s:2 * s + 2, 0:NB]

        def scr(i, a=HALO, b=ROW):
            return SCR[:, i, a:b]

        rr = {"ps": 0, "scr": 0}
        reserved = set()

        def next_ps():
            while True:
                s = rr["ps"]
                rr["ps"] = (s + 1) % 4
                if s not in reserved:
                    return s

        def next_scr():
            s = rr["scr"]
            rr["scr"] = (s + 1) % NROT
            return s

        def kscr(i):
            return [("scr", i), ("scrh", i)]

        def src_cols(wd, a, b):
            return wd[:, a:b].rearrange("(k p) o -> p k o", p=128)

        units = []

        def add_unit(nk, ntot, parts):
            units.append((nk, ntot, parts))

        for t in range(NTILES):
            add_unit(8, 512, [(0, 512, src_cols(w_inu_d, 0, 512))])
            for j in range(4):
                add_unit(8, 384, [(0, 384, w_inc_d[j].rearrange("(k p) o -> p k o", p=128))])
            for ou in range(2):
                add_unit(8, 512, [(0, 512, src_cols(w_out_d, 512 * ou, 512 * ou + 512))])
            for item in ffn_sequence():
                if item[0] == "up":
                    jj = item[2]
                    add_unit(8, 512, [(0, 512, w_upu_d[jj // 2].rearrange("(k p) o -> p k o", p=128))])
                else:
                    j0, j1 = PARTS3[item[1]]
                    nk = j1 - j0
                    for ou in range(2):
                        add_unit(nk, 512, [(0, 512, w_down_d[128 * j0:128 * j1, 512 * ou:512 * ou + 512]
                                            .rearrange("(k p) o -> p k o", p=128))])
            add_unit(8, 512, [(0, 512, src_cols(w_gate_d, 0, 512))])
            add_unit(2, 1024, [(0, 1024, src_cols(w_proj_d, 0, 1024))])
            add_unit(8, 512, [(0, 512, src_cols(w_gate_d, 512, 1024))])

        ustate = {"issued": 0, "next": 0, "done": set()}

        def uview(i):
            nk, ntot, _ = units[i]
            return WS[:, i % NSLOT, 0:nk * ntot].rearrange("p (k o) -> p k o", k=nk)

        def ukeys(i):
            return [("w", i % NSLOT, q) for q in range(3)]

        def issue_units(limit=None):
            while ustate["issued"] < len(units):
                if limit is not None and ustate["issued"] >= limit:
                    break
                i = ustate["issued"]
                if i >= NSLOT and (i - NSLOT) not in ustate["done"]:
                    break
                nk, ntot, parts = units[i]
                v = uview(i)
                slot = i % NSLOT
                for q, (off, ncol, src) in enumerate(parts):
                    if len(parts) == 1:
                        keys = ukeys(i)
                    elif q == len(parts) - 1:
                        keys = [("w", slot, qq) for qq in range(q, 3)]
                    else:
                        keys = [("w", slot, q)]
                    dst = v[:, :, off:off + ncol]
                    P.dma("pool", "semW%d_%d" % (slot, q),
                          (lambda e, dst=dst, src=src: e.dma_start(out=dst, in_=src)),
                          r=(), w=keys)
                ustate["issued"] += 1

        def take_unit():
            i = ustate["next"]
            ustate["next"] += 1
            assert i < ustate["issued"], "unit not issued (prefetch logic)"
            return i

        def unit_done(i):
            ustate["done"].add(i)
            issue_units()

        def mm_group(s, lhs_fn, rhs_fn, nk, rkeys):
            items = []
            for b in range(2):
                for k in range(nk):
                    items.append((PS[:, 2 * s + b, 0:NB], lhs_fn(k), rhs_fn(k, b), k == 0, k == nk - 1))

            def fn(e):
                ins = None
                for (o, l, r_, st, sp) in items:
                    ins = e.matmul(o, l, r_, start=st, stop=sp)
                return ins
            P.op("pe", fn, r=rkeys, w=[("ps", s)])

        def mm_kouter(slots, lhs_fns, rhs_fn, nk, rkeys_k, rkeys_common):
            for k in range(nk):
                items = []
                for ci, s in enumerate(slots):
                    for b in range(2):
                        items.append((PS[:, 2 * s + b, 0:NB], lhs_fns[ci](k), rhs_fn(k, b), k == 0, k == nk - 1))

                def fn(e, items=items):
                    ins = None
                    for (o, l, r_, st, sp) in items:
                        ins = e.matmul(o, l, r_, start=st, stop=sp)
                    return ins
                P.op("pe", fn, r=[rkeys_k(k)] + rkeys_common, w=[("ps", s) for s in slots])

        def act(out, in_, func, r, w, **kw):
            P.op("act", (lambda e: e.activation(out=out, in_=in_, func=func, **kw)), r=r, w=w)

        def tt(out, in0, in1, op, r, w):
            P.op("dve", (lambda e: e.tensor_tensor(out=out, in0=in0, in1=in1, op=op)), r=r, w=w)

        def stt(out, in0, scalar, in1, op0, op1, r, w):
            P.op("dve", (lambda e: e.scalar_tensor_tensor(out=out, in0=in0, scalar=scalar, in1=in1,
                                                          op0=op0, op1=op1)), r=r, w=w)

        def pcopy(out, in_, r, w):
            P.op("pool", (lambda e: e.tensor_copy(out=out, in_=in_)), r=r, w=w)

        P.dma("sp", "semC", lambda e: e.dma_start(out=VEC[:, :], in_=vec_d), w=["VEC"])
        for k in range(8):
            P.dma("sp", "semX%d" % k, (lambda e, k=k: e.dma_start(
                out=RB[0][:, k, :], in_=xT[128 * k:128 * k + 128, 0:NT])), w=[("R", 0, k)])
        P.op("pool", lambda e: e.memset(ONESF[:, :], 1.0), w=["ONESF"])
        P.op("pool", lambda e: e.memset(EPSV[:, :], EPS), w=["EPSV"])
        P.op("act", lambda e: e.activation(out=ONES[:, :], in_=ONESF[:, :], func=AF.Copy),
             r=["ONESF"], w=["ONES"])
        issue_units(limit=1)
        P.dma("pool", "semWP", lambda e: e.dma_start(out=WP[:, :, :],
                                                      in_=w_pool_d.rearrange("g c d -> c g d")), w=["WP"])
        P.op("pool", lambda e: e.memset(UPL[:, :, :], 0.0), w=[("UPL", c) for c in range(44)])
        P.op("pool", lambda e: e.memset(CHL[:, :, :], 0.0), w=[("CHL", c) for c in range(4)])
        P.op("pool", lambda e: e.memset(UL[:, :, :], 0.0), w=[("UL", c) for c in range(4)])
        for g, wdw in enumerate(POOL_W):
            for tcol in range(16):
                val = 1.0 / min(wdw, tcol + 1)
                P.op("pool", (lambda e, g=g, tcol=tcol, val=val: e.memset(INVC[:, g, tcol:tcol + 1], val)),
                     w=[("INVC", g)])
        issue_units()
        P.dma("pool", "semPT", lambda e: e.dma_start(
            out=PT[:, :, :], in_=pT[:, 0:NT].rearrange("(k p) n -> p k n", p=128)),
            w=[("PT", 0), ("PT", 1)])
        P.dma("sp", "semC", lambda e: e.dma_start(out=SPs[:, :, :, :], in_=sp_d), w=["SPs"])
        P.dma("sp", "semC", lambda e: e.dma_start(out=SCs[:, :, :, :], in_=sc_d), w=["SCs"])
        P.dma("sp", "semC", lambda e: e.dma_start(out=SFs[:, :, :, :], in_=sf_d), w=["SFs"])
        for kk in ("VEC", "SPs", "SCs", "SFs"):
            P.last_write[kk] = ("semC", P.count["semC"], "dma:semC")
        P.dma("sp", "semO", lambda e: e.dma_start(out=nfs_d[:, 0, :, :], in_=sf_d[:, 1, :, :]))

        def build_precompute_pieces():
            pieces = []

            def p_pools():
                act(PACC[:, :, :], SPs[:, 14, :, :], AF.Copy, r=["SPs"], w=["PACC"])
                act(POOLS[:, 0, :], PACC[:, 0, :], AF.Copy, r=["PACC"], w=[("POOLS", 0)])
            pieces.append(p_pools)
            row = 13
            for g in range(1, 4):
                lo = 15 - (POOL_W[g] - 1)
                rows = list(range(row, lo - 1, -1))
                row = lo - 1

                def p_g(g=g, rows=rows):
                    for rw in rows:
                        tt(PACC[:, :, :], PACC[:, :, :], SPs[:, rw, :, :], ALU.add, r=["PACC", "SPs"], w=["PACC"])
                    act(POOLS[:, g, :], PACC[:, g, :], AF.Copy, r=["PACC"], w=[("POOLS", g)])
                pieces.append(p_g)

            def p_c(j):
                act(BASEC[:, j, :], SCs[:, 0, j, :], AF.Identity, r=["SCs", "VEC"], w=[("CHS_base", j)],
                    scale=vcol(CW + j))
                stt(BASEC[:, j, :], SCs[:, 1, j, :], vcol(CW + 4 + j), BASEC[:, j, :], ALU.mult, ALU.add,
                    r=["SCs", "VEC", ("CHS_base", j)], w=[("CHS_base", j)])

            def p_s(c):
                act(BASES[:, c, :], SFs[:, 0, c, :], AF.Identity, r=["SFs", "VEC"], w=[("UPS_base", c)],
                    scale=vcol(FW + c))
                stt(BASES[:, c, :], SFs[:, 1, c, :], vcol(FW + 44 + c), BASES[:, c, :], ALU.mult, ALU.add,
                    r=["SFs", "VEC", ("UPS_base", c), ("UPS", c)], w=[("UPS_base", c)])
            for j in range(4):
                pieces.append(lambda j=j: p_c(j))
            for c in range(44):
                pieces.append(lambda c=c: p_s(c))
            return pieces

        pre_pieces = build_precompute_pieces()

        class NormAcc:
            def __init__(self, rb, sq_fn, maxpend=8):
                self.rb = rb
                self.maxpend = maxpend
                self.sq_fn = sq_fn
                self.s = next_ps()
                reserved.add(self.s)
                self.cnt = 0
                self.pending = []

            def square(self, k):
                while len(self.pending) >= self.maxpend:
                    self.mm()
                ap, key = self.sq_fn(k)
                act(ap, RB[self.rb][:, k, :], AF.Square, r=[("R", self.rb, k)], w=[key])
                self.pending.append(k)

            def mm(self):
                if not self.pending:
                    return
                k = self.pending.pop(0)
                ap, key = self.sq_fn(k)
                first, last = self.cnt == 0, self.cnt == 7
                self.cnt += 1
                s = self.s

                def fn(e):
                    ins = None
                    for b in range(2):
                        ins = e.matmul(PS[:, 2 * s + b, 0:NB], ONES[:, :], ap[:, b * NB:(b + 1) * NB],
                                       start=first, stop=last)
                    return ins
                P.op("pe", fn, r=[key, "ONES"], w=[("ps", s)])

            def stash(self, row):
                while self.pending:
                    self.mm()
                assert self.cnt == 8
                act(blk2(scr(row)), psv(self.s), AF.Copy, r=[("ps", self.s)], w=[("scr", row)])
                reserved.discard(self.s)

            def finish(self, rs):
                while self.pending:
                    self.mm()
                assert self.cnt == 8
                sd = next_scr()
                act(blk2(scr(sd)), psv(self.s), AF.Ln, r=[("ps", self.s), "EPSV"], w=[("scr", sd)],
                    bias=EPSV[:, 0:1], scale=1.0 / 1024.0)
                reserved.discard(self.s)
                act(scr(rs), scr(sd), AF.Exp, r=[("scr", sd)], w=[("scr", rs)], scale=-0.5)
                return rs

        def preload_ln_table():
            act(DUMMY[:, 0:1], EPSV[:, 0:1], AF.Ln, r=["EPSV"], w=["DUMMY"])

        def preload_table(func):
            act(DUMMY[:, 1:2], EPSV[:, 0:1], func, r=["EPSV"], w=["DUMMY1"])

        def ln_exp_inplace(row):
            act(scr(row), scr(row), AF.Ln, r=[("scr", row), "EPSV"], w=[("scr", row)],
                bias=EPSV[:, 0:1], scale=1.0 / 1024.0)
            act(scr(row), scr(row), AF.Exp, r=[("scr", row)], w=[("scr", row)], scale=-0.5)

        def norm_apply(rb, gcol, rs, to_y=False):
            for k in range(8):
                out = RB[rb][:, k, :] if to_y else H[:, k, :]
                stt(out, RB[rb][:, k, :], vcol(gcol + k), scr(rs), ALU.mult, ALU.mult,
                    r=[("R", rb, k), ("scr", rs), "VEC"], w=[("R", rb, k) if to_y else ("H", k)])

        def sq_H(k):
            return H[:, k, :], ("H", k)

        def sq_GB(k):
            return GB[:, k, :], ("GB", k)

        def sq_D(k):
            return DROW[:, k % 2, :], ("DROW", k % 2)

        def conv_from_row(src, c, wcol, bcol, nch, base_t, key_l, L, key_s, S_, npr, has_s):
            pcopy(SCR[:, src, HALO - 2:HALO], L[:, c, :], r=[(key_l, c)], w=[("scrh", src)])
            pcopy(L[:, c, :], SCR[:, src, HALO + npr - 2:HALO + npr], r=[("scr", src)], w=[(key_l, c)])
            if has_s:
                pcopy(S_[:, c, :], SCR[:, src, HALO + npr:ROW], r=[("scr", src)], w=[(key_s, c)])
            acc = next_scr()
            act(scr(acc), scr(src), AF.Identity, r=[("scr", src), "VEC"], w=[("scr", acc)],
                scale=vcol(wcol + 2 * nch + c), bias=vcol(bcol + c))
            stt(SCR[:, acc, HALO:HALO + npr], SCR[:, src, HALO - 1:HALO + npr - 1], vcol(wcol + nch + c),
                SCR[:, acc, HALO:HALO + npr], ALU.mult, ALU.add,
                r=kscr(src) + [("scr", acc), "VEC"], w=[("scr", acc)])
            stt(SCR[:, acc, HALO:HALO + npr], SCR[:, src, HALO - 2:HALO + npr - 2], vcol(wcol + c),
                SCR[:, acc, HALO:HALO + npr], ALU.mult, ALU.add,
                r=kscr(src) + [("scr", acc), "VEC"], w=[("scr", acc)])
            if has_s:
                tt(SCR[:, acc, HALO + npr:ROW], SCR[:, acc, HALO + npr:ROW], base_t[:, c, :], ALU.add,
                   r=[("scr", acc), (key_s + "_base", c)], w=[("scr", acc)])
            return acc

        state = {"rs1": None, "n4": None, "ytodo": []}

        def proj_residual(rb, nk, src_key, src_fn, norm_make, hg_col=None):
            R = RB[rb]
            u0 = take_unit()
            W0 = uview(u0)
            u1 = None
            W1 = None
            skeys = [(src_key, k) for k in range(nk)]
            KO = 3
            sl = [next_ps() for _ in range(KO)]
            mm_kouter(sl, [(lambda k, q=q: W0[:, k, q * 128:(q + 1) * 128]) for q in range(KO)],
                      src_fn, nk, rkeys_k=lambda k: (src_key, k), rkeys_common=ukeys(u0))
            na = None
            for oc in range(8):
                q = oc % 4
                if oc == 4:
                    unit_done(u0)
                    u1 = take_unit()
                    W1 = uview(u1)
                if oc < KO:
                    s = sl[oc]
                else:
                    Wc_, uc_ = (W0, u0) if oc < 4 else (W1, u1)
                    s = next_ps()
                    mm_group(s, lambda k, q=q, Wc_=Wc_: Wc_[:, k, q * 128:(q + 1) * 128], src_fn, nk,
                             rkeys=skeys + ukeys(uc_))
                if na is not None and oc >= KO:
                    na.mm()
                    if len(na.pending) > 2:
                        na.mm()
                tt(blk2(R[:, oc, :]), psv(s), blk2(R[:, oc, :]), ALU.add,
                   r=[("ps", s), ("R", rb, oc)], w=[("R", rb, oc)])
                if na is None and norm_make is not None and oc >= 1:
                    na = norm_make()
                    for o2 in range(oc):
                        na.square(o2)
                if na is not None:
                    na.square(oc)
                if hg_col is not None:
                    act(H[:, oc, :], R[:, oc, :], AF.Identity, r=[("R", rb, oc), "VEC"], w=[("H", oc)],
                        scale=vcol(hg_col + oc))
                if pre_pieces and oc < 7:
                    pre_pieces.pop(0)()
            unit_done(u1)
            return na

        def emit_state_outputs():
            def out_dma(dst, src, r):
                P.dma("sp", "semO", (lambda e: e.dma_start(out=dst, in_=src)), r=r)

            out_dma(npp_d, UL[:, :, :], [("UL", g) for g in range(4)])
            out_dma(ncp_d, CHL[:, :, :], [("CHL", g) for g in range(4)])
            out_dma(nfp_d, UPL[:, :, :], [("UPL", c) for c in range(44)])
            out_dma(nps_d[:, 0:14, :, :], SPs[:, 1:15, :, :], ["SPs"])
            out_dma(nps_d[:, 14, :, :], US[:, :, :], [("US", g) for g in range(4)])
            out_dma(ncs_d[:, 0, :, :], SCs[:, 1, :, :], ["SCs"])
            out_dma(ncs_d[:, 1, :, :], CHS[:, :, :], [("CHS", g) for g in range(4)])
            out_dma(nfs_d[:, 1, :, :], UPS, [("UPS", c) for c in range(44)])

        def tile_head(t):
            rb = t % 2
            if t == 0:
                na = NormAcc(0, sq_H)
                for k in range(8):
                    na.square(k)
                    na.mm()
                na.finish(RSTD1)
            norm_apply(rb, GM, RSTD1)

        def tile_tail(t):
            rb = t % 2
            c0 = NT * t
            rs = next_scr()
            state["n4"].finish(rs)
            for k in range(8):
                stt(RB[rb][:, k, :], RB[rb][:, k, :], vcol(GFIN + k), scr(rs), ALU.mult, ALU.mult,
                    r=[("R", rb, k), ("scr", rs), "VEC"], w=[("R", rb, k)])
                P.dma("sp", "semX%d" % k, (lambda e, k=k: e.dma_start(
                    out=yT[128 * k:128 * k + 128, c0:c0 + NT], in_=RB[rb][:, k, :])),
                    r=[("R", rb, k)])

        def tail_start(t):
            state["n4"].finish(RSTD1)
            state["ytodo"] = [(t, k) for k in range(8)]

        def tail_piece(n):
            for _ in range(n):
                if not state["ytodo"]:
                    return
                t, k = state["ytodo"].pop(0)
                rb = t % 2
                stt(RB[rb][:, k, :], RB[rb][:, k, :], vcol(GFIN + k), scr(RSTD1), ALU.mult, ALU.mult,
                    r=[("R", rb, k), ("scr", RSTD1), "VEC"], w=[("R", rb, k)])
                if not state["ytodo"]:
                    c0 = NT * t
                    P.dma("sp", "semY", lambda e: e.dma_start(
                        out=yT[:, c0:c0 + NT].rearrange("(k p) n -> p k n", p=128), in_=RB[rb][:, :, :]),
                        r=[("R", rb, k) for k in range(8)])

        def tile_body(t):
            rb = t % 2
            R = RB[rb]
            has_s = (t == NTILES - 1)
            npr = NT - NS if has_s else NT
            hk = [("H", k) for k in range(8)]

            def rk(k):
                return ("R", rb, k)

            def hrhs(k, b):
                return H[:, k, b * NB:(b + 1) * NB]

            ui = take_unit()
            Wu = uview(ui)
            uslots = [next_ps() for _ in range(4)]
            mm_kouter(uslots, [(lambda k, g=g: Wu[:, k, g * 128:(g + 1) * 128]) for g in range(4)],
                      hrhs, 8, rkeys_k=lambda k: ("H", k), rkeys_common=ukeys(ui))
            unit_done(ui)
            for s_ in uslots:
                reserved.add(s_)

            def pool_chain(g):
                wdw = POOL_W[g]
                s = uslots[g]
                U = next_scr()
                act(blk2(scr(U)), psv(s), AF.Copy, r=[("ps", s)], w=[("scr", U)])
                reserved.discard(s)
                pcopy(SCR[:, U, 1:HALO], UL[:, g, :], r=[("UL", g)], w=[("scrh", U)])
                pcopy(UL[:, g, :], SCR[:, U, HALO + npr - 15:HALO + npr], r=[("scr", U)], w=[("UL", g)])
                if has_s:
                    pcopy(US[:, g, :], SCR[:, U, HALO + npr:ROW], r=[("scr", U)], w=[("US", g)])
                A = next_scr()
                B = next_scr()
                cur, sh, lo = U, 1, 2
                dst = A
                for lvl in range(g + 1):
                    tt(SCR[:, dst, lo:ROW], SCR[:, cur, lo:ROW], SCR[:, cur, lo - sh:ROW - sh], ALU.add,
                       r=kscr(cur), w=kscr(dst))
                    cur = dst
                    dst = B if cur == A else A
                    sh *= 2
                    lo *= 2
                Srow = cur
                di = g
                stt(GB[:, g, :], scr(Srow), 1.0 / wdw, scr(U), ALU.mult, ALU.subtract,
                    r=kscr(Srow) + kscr(U), w=[("GB", g)])
                tmp = dst
                if t == 0:
                    tt(SCR[:, tmp, 0:15], SCR[:, Srow, HALO:HALO + 15], INVC[:, g, 0:15], ALU.mult,
                       r=kscr(Srow) + [("INVC", g)], w=kscr(tmp))
                    tt(GB[:, g, 0:15], SCR[:, tmp, 0:15], SCR[:, U, HALO:HALO + 15], ALU.subtract,
                       r=kscr(tmp) + kscr(U) + [("GB", g)], w=[("GB", g)])
                if has_s:
                    tt(SCR[:, tmp, 0:NS], SCR[:, U, HALO + npr:ROW], POOLS[:, g, :], ALU.add,
                       r=kscr(U) + [("POOLS", g)], w=kscr(tmp))
                    stt(GB[:, g, npr:NT], SCR[:, tmp, 0:NS], 1.0 / wdw, SCR[:, U, HALO + npr:ROW],
                        ALU.mult, ALU.subtract, r=kscr(tmp) + kscr(U) + [("GB", g)], w=[("GB", g)])
                return di

            def pool_mm(g, di):
                s2 = next_ps()
                mm_group(s2, lambda k: WP[:, g, :], lambda k, b: GB[:, g, b * NB:(b + 1) * NB], 1,
                         rkeys=[("GB", g), "WP"])
                act(blk2(GB[:, g, :]), psv(s2), AF.Identity, r=[("ps", s2), "VEC"], w=[("GB", g)],
                    scale=vcol(PSC + g))

            def conv_unit(j):
                ci = take_unit()
                Wc = uview(ci)
                s_cg = next_ps()
                mm_group(s_cg, lambda k: Wc[:, k, 128:256], hrhs, 8, rkeys=hk + ukeys(ci))
                s_hv = next_ps()
                mm_group(s_hv, lambda k: Wc[:, k, 256:384], hrhs, 8, rkeys=hk + ukeys(ci))
                CG = next_scr()
                act(blk2(scr(CG)), psv(s_cg), AF.Copy, r=[("ps", s_cg)], w=[("scr", CG)])
                s_bg = next_ps()
                mm_group(s_bg, lambda k: Wc[:, k, 0:128], hrhs, 8, rkeys=hk + ukeys(ci))
                CH = next_scr()
                tt(blk2(scr(CH)), psv(s_hv), blk2(scr(CG)), ALU.mult, r=[("ps", s_hv), ("scr", CG)],
                   w=[("scr", CH)])
                acc = conv_from_row(CH, j, CW, CB, 4, BASEC, "CHL", CHL, "CHS", CHS, npr, has_s)
                tt(blk2(GB[:, 4 + j, :]), psv(s_bg), blk2(scr(acc)), ALU.mult,
                   r=[("ps", s_bg), ("scr", acc)], w=[("GB", 4 + j)])
                unit_done(ci)

            d0 = pool_chain(0)
            d1 = pool_chain(1)
            conv_unit(0)
            pool_mm(0, d0)
            pool_mm(1, d1)
            tail_piece(2)
            d2 = pool_chain(2)
            d3 = pool_chain(3)
            conv_unit(1)
            tail_piece(2)
            conv_unit(2)
            pool_mm(2, d2)
            pool_mm(3, d3)
            tail_piece(2)
            conv_unit(3)
            tail_piece(2)
            n2 = proj_residual(rb, 8, "GB", lambda k, b: GB[:, k, b * NB:(b + 1) * NB],
                               lambda: NormAcc(rb, sq_H))
            rs2 = next_scr()
            n2.finish(rs2)
            preload_table(AF.Silu)
            norm_apply(rb, GF, rs2)

            if t + 1 < NTILES:
                nb_ = (t + 1) % 2
                c1 = NT * (t + 1)
                P.dma("sp", "semR%d" % nb_, lambda e: e.dma_start(
                    out=RB[nb_][:, :, :], in_=xT[:, c1:c1 + NT].rearrange("(k p) n -> p k n", p=128)),
                    w=[("R", nb_, k) for k in range(8)])

            parts3 = PARTS3
            n1 = None
            n3 = None
            deferred = []

            def flush_deferred():
                while deferred:
                    deferred.pop(0)()

            def up_unit(pi, jj, pulled):
                j0 = parts3[pi][0]
                fi = take_unit()
                Wf = uview(fi)
                lhs4 = [(lambda k, q=q: Wf[:, k, q * 128:(q + 1) * 128]) for q in range(2)] + \
                       [(lambda k, q=q: Wf[:, k, 256 + q * 128:256 + (q + 1) * 128]) for q in range(2)]
                if jj == 0:
                    sl2 = [next_ps() for _ in range(2)]
                    mm_kouter(sl2, [lhs4[0], lhs4[2]], hrhs, 8,
                              rkeys_k=lambda k: ("H", k), rkeys_common=ukeys(fi))
                    pre = [(sl2[0], sl2[1])]
                else:
                    pre = None
                for q in range(2):
                    if pre is not None and q < len(pre):
                        s_a, s_v = pre[q]
                    else:
                        s_a = next_ps()
                        mm_group(s_a, lhs4[q], hrhs, 8, rkeys=hk + ukeys(fi))
                        s_v = next_ps()
                        mm_group(s_v, lhs4[2 + q], hrhs, 8, rkeys=hk + ukeys(fi))
                    j = jj + q
                    accs = []
                    for (s_x, c) in ((s_a, j), (s_v, 22 + j)):
                        UP = next_scr()
                        act(blk2(scr(UP)), psv(s_x), AF.Copy, r=[("ps", s_x)], w=[("scr", UP)])
                        accs.append(conv_from_row(UP, c, FW, FB, 44, BASES, "UPL", UPL, "UPS", UPS,
                                                  npr, has_s))
                    a_, v_ = accs
                    act(scr(a_), scr(a_), AF.Silu, r=[("scr", a_)], w=[("scr", a_)])
                    if j == 21:
                        preload_ln_table()

                    def gate_mul(a_=a_, v_=v_, jo=j - j0):
                        tt(GB[:, jo, :], scr(a_), scr(v_), ALU.mult,
                           r=[("scr", a_), ("scr", v_)], w=[("GB", jo)])
                    if pulled:
                        deferred.append(gate_mul)
                    else:
                        gate_mul()
                unit_done(fi)

            for item in ffn_sequence():
                if item[0] == "up":
                    up_unit(item[1], item[2], item[3])
                    continue
                pi = item[1]
                j0, j1 = parts3[pi]
                nk = j1 - j0
                mk = None
                if pi == 1 and t + 1 < NTILES:
                    mk = lambda: NormAcc((t + 1) % 2, sq_D, maxpend=2)
                if pi == 2:
                    mk = lambda: NormAcc(rb, sq_D, maxpend=2)
                na_ = proj_residual(rb, nk, "GB", lambda k, b: GB[:, k, b * NB:(b + 1) * NB], mk,
                                    hg_col=(GP if pi == 2 else None))
                if pi == 1:
                    n1 = na_
                if pi == 2:
                    n3 = na_
                if pi == 1 and n1 is not None:
                    n1.stash(RSTD1)
                    state["n1_pending"] = True
                flush_deferred()
            rs3 = next_scr()
            n3.finish(rs3)
            preload_table(AF.Sigmoid)
            if t == NTILES - 1:
                emit_state_outputs()
            assert not (t == 1 and pre_pieces), "precompute pieces left over"

            g0 = take_unit()
            pj = take_unit()
            g1 = take_unit()
            Wpj = uview(pj)
            Wg0 = uview(g0)
            Wg1 = uview(g1)
            ppk = [("PT", 0), ("PT", 1)] + ukeys(pj)
            KG = 3
            gs4 = [next_ps() for _ in range(KG)]
            mm_kouter(gs4, [(lambda k, q=q: Wg0[:, k, q * 128:(q + 1) * 128]) for q in range(KG)], hrhs, 8,
                      rkeys_k=lambda k: ("H", k), rkeys_common=ukeys(g0))
            gts = [next_scr() for _ in range(4)]
            for q in range(KG):
                GT = gts[q]
                tt(blk2(scr(GT)), psv(gs4[q]), blk2(scr(rs3)), ALU.mult, r=[("ps", gs4[q]), ("scr", rs3)],
                   w=[("scr", GT)])
                act(scr(GT), scr(GT), AF.Sigmoid, r=[("scr", GT)], w=[("scr", GT)])
            n4 = None
            for oc in range(8):
                if oc == 4:
                    unit_done(g0)
                if oc >= KG:
                    q = oc % 4
                    Wg_, ug_ = (Wg0, g0) if oc < 4 else (Wg1, g1)
                    sg = next_ps()
                    mm_group(sg, lambda k, q=q, Wg_=Wg_: Wg_[:, k, q * 128:(q + 1) * 128], hrhs, 8,
                             rkeys=hk + ukeys(ug_))
                sp_ = next_ps()
                mm_group(sp_, lambda k, oc=oc: Wpj[:, k, oc * 128:(oc + 1) * 128],
                         lambda k, b: PT[:, k, b * NB:(b + 1) * NB], 2, rkeys=ppk)
                if n4 is not None and oc >= 3:
                    n4.mm()
                GT = gts[oc % 4]
                if oc >= KG:
                    tt(blk2(scr(GT)), psv(sg), blk2(scr(rs3)), ALU.mult, r=[("ps", sg), ("scr", rs3)],
                       w=[("scr", GT)])
                    act(scr(GT), scr(GT), AF.Sigmoid, r=[("scr", GT)], w=[("scr", GT)])
                    if oc == 7:
                        preload_ln_table()
                        if state.get("n1_pending"):
                            ln_exp_inplace(RSTD1)
                            state["n1_pending"] = False
                tt(blk2(scr(GT)), psv(sp_), blk2(scr(GT)), ALU.mult, r=[("ps", sp_), ("scr", GT)],
                   w=[("scr", GT)])
                tt(R[:, oc, :], R[:, oc, :], scr(GT), ALU.add, r=[rk(oc), ("scr", GT)], w=[rk(oc)])
                if n4 is None and oc >= 1:
                    n4 = NormAcc(rb, sq_GB)
                    state["n4"] = n4
                    for o2 in range(oc):
                        n4.square(o2)
                if n4 is not None:
                    n4.square(oc)
            unit_done(pj)
            unit_done(g1)
            if t + 1 < NTILES:
                c1 = NT * (t + 1)
                P.dma("pool", "semPT", lambda e: e.dma_start(
                    out=PT[:, :, :], in_=pT[:, c1:c1 + NT].rearrange("(k p) n -> p k n", p=128)),
                    w=[("PT", 0), ("PT", 1)])

        for t in range(NTILES):
            tile_head(t)
            if t > 0:
                tail_start(t - 1)
            tile_body(t)
        tile_tail(NTILES - 1)

        final_waits = [("semO", P.count["semO"]), ("semY", P.count["semY"])] + \
                      [("semX%d" % k, P.count["semX%d" % k]) for k in range(8)]

        with nc.Block() as block:
            def emit(eng_name, e, extra_waits=()):
                for (waits, fn, semname, inc) in P.streams[eng_name]:
                    for (s, v) in waits:
                        e.wait_ge(sems[s], v)
                    ins = fn(e)
                    ins.then_inc(sems[semname], inc)
                for (s, v) in extra_waits:
                    e.wait_ge(sems[s], v)

            @block.tensor
            def _(e):
                emit("pe", e)

            @block.scalar
            def _(e):
                emit("act", e)

            @block.vector
            def _(e):
                emit("dve", e)

            @block.gpsimd
            def _(e):
                emit("pool", e)

            @block.sync
            def _(e):
                emit("sp", e, final_waits)
    return nc


def _pack_vec(inp):
    v = np.zeros((128, NV), np.float32)
    v[:, GM:GM + 8] = inp["g_mix"][0].reshape(8, 128).T
    v[:, GF:GF + 8] = inp["g_ffn"][0].reshape(8, 128).T
    v[:, GP:GP + 8] = inp["g_ple"][0].reshape(8, 128).T
    v[:, GFIN:GFIN + 8] = inp["g_final"].reshape(8, 128).T
    v[:, PSC:PSC + 4] = inp["pool_scale"][0].reshape(4, 128).T
    for k in range(3):
        v[:, CW + 4 * k:CW + 4 * k + 4] = inp["conv_w"][0][k].reshape(4, 128).T
        v[:, FW + 44 * k:FW + 44 * k + 44] = inp["ffn_conv_w"][0][k].reshape(44, 128).T
    v[:, CB:CB + 4] = inp["conv_b"][0].reshape(4, 128).T
    v[:, FB:FB + 44] = inp["ffn_conv_b"][0].reshape(44, 128).T
    return v


def kernel(**inp):
    inp = {k: np.asarray(v) for k, v in inp.items()}
    f = np.float32
    vec = _pack_vec(inp)
    w_in = inp["w_in"][0]
    w_up = inp["w_up"][0]
    shared = {
        "vec": vec,
        "w_in_u": np.ascontiguousarray(w_in[:, 0:512], f),
        "w_in_c": np.ascontiguousarray(np.stack([np.concatenate(
            [w_in[:, 512 + 128 * j:640 + 128 * j], w_in[:, 1024 + 128 * j:1152 + 128 * j],
             w_in[:, 1536 + 128 * j:1664 + 128 * j]], axis=1) for j in range(4)]), f),
        "w_pool": np.ascontiguousarray(inp["w_pool"][0], f),
        "w_out": np.ascontiguousarray(inp["w_out"][0], f),
        "w_up_u": np.ascontiguousarray(np.stack([np.concatenate(
            [w_up[:, 256 * i:256 * i + 256], w_up[:, 2816 + 256 * i:2816 + 256 * i + 256]], axis=1)
            for i in range(11)]), f),
        "w_down": np.ascontiguousarray(inp["w_down"][0], f),
        "w_gate": np.ascontiguousarray(inp["w_ple_gate"][0], f),
        "w_proj": np.ascontiguousarray(inp["w_ple_proj"][0], f),
    }
    in_maps = []
    for b in range(NCORES):
        sl = slice(NS * b, NS * b + NS)
        xT = np.concatenate([inp["x_prompt"][b].T, inp["x_sample"][sl, 0].T], axis=1)
        pT = np.concatenate([inp["p_prompt"][0, b].T, inp["p_sample"][0, sl, 0].T], axis=1)
        spv = inp["state_pool"][0, sl].reshape(NS, 15, 4, 128).transpose(3, 1, 2, 0)
        scv = inp["state_conv"][0, sl].reshape(NS, 2, 4, 128).transpose(3, 1, 2, 0)
        sfv = inp["state_ffn"][0, sl].reshape(NS, 2, 44, 128).transpose(3, 1, 2, 0)
        m = dict(shared)
        m.update({
            "xT": np.ascontiguousarray(xT, f), "pT": np.ascontiguousarray(pT, f),
            "sp_in": np.ascontiguousarray(spv, f), "sc_in": np.ascontiguousarray(scv, f),
            "sf_in": np.ascontiguousarray(sfv, f),
        })
        in_maps.append(m)

    nc = build_program()
    res = run_bass_kernel_spmd(nc, in_maps, core_ids=list(range(NCORES)))
    outs = res.results

    y_prompt = np.empty((8, NPROMPT, 1024), f)
    y_sample = np.empty((128, 1, 1024), f)
    npp = np.empty((1, 8, 15, 512), f)
    ncp = np.empty((1, 8, 2, 512), f)
    nfp = np.empty((1, 8, 2, 5632), f)
    nps = np.empty((1, 128, 15, 512), f)
    ncs = np.empty((1, 128, 2, 512), f)
    nfs = np.empty((1, 128, 2, 5632), f)
    for b in range(NCORES):
        o = outs[b]
        sl = slice(NS * b, NS * b + NS)
        yT = o["yT"]
        y_prompt[b] = yT[:, :NPROMPT].T
        y_sample[sl, 0] = yT[:, NPROMPT:].T
        npp[0, b] = o["npp"].transpose(2, 1, 0).reshape(15, 512)
        ncp[0, b] = o["ncp"].transpose(2, 1, 0).reshape(2, 512)
        nfp[0, b] = o["nfp"].transpose(2, 1, 0).reshape(2, 5632)
        nps[0, sl] = o["nps"].transpose(3, 1, 2, 0).reshape(NS, 15, 512)
        ncs[0, sl] = o["ncs"].transpose(3, 1, 2, 0).reshape(NS, 2, 512)
        nfs[0, sl] = o["nfs"].transpose(3, 1, 2, 0).reshape(NS, 2, 5632)
    return (y_prompt, y_sample, npp, ncp, nfp, nps, ncs, nfs)
```
